# Optimizing a Trainium2 kernel written in Bass

```python
import jax, jax.numpy as jnp
from jax import lax
import numpy as np

D_MODEL = 1024
BATCH = 8
SEQ = 2048
DEPTH = 2
DEC_BATCH = 128
DEC_SEQ = 4
PAST_LEN = 16384
PAGE_SIZE = 128

EXPAND = 2
MIX_WIDTH = EXPAND * D_MODEL
RW_HEAD = 64
RW_HEADS = MIX_WIDTH // RW_HEAD
LORA_W = 64
LORA_A = 64
SHIFT_COLS = 3 * MIX_WIDTH + LORA_W + LORA_A
RW_IN_COLS = SHIFT_COLS + MIX_WIDTH
GN_EPS = 64e-5
CHUNK = 128
GM_GROUPS = 8
GM_GROUP_DIM = MIX_WIDTH // GM_GROUPS
GM_IN_COLS = 3 * MIX_WIDTH
N_RWKV = (DEPTH + 1) // 2
N_GMLP = DEPTH // 2
NORM_EPS = 1e-6
LN_EPS = 1e-5

kernel_name = "hybrid_rwkv7_gmlp_decode_step"


def _rmsnorm(x, g):
    xf = x.astype(jnp.float32)
    y = xf * lax.rsqrt(jnp.mean(xf * xf, axis=-1, keepdims=True) + NORM_EPS)
    return y.astype(x.dtype) * g


def _wkv7_scan(r, decay, k, v, kk, a, s0):
    xs = tuple(jnp.swapaxes(arr, 0, 1) for arr in (r, decay, k, v, kk, a))

    def step(S, inp):
        r_t, w_t, k_t, v_t, kk_t, a_t = inp
        sa = jnp.einsum('bhvk,bhk->bhv', S, -kk_t)
        S = (S * w_t[:, :, None, :]
             + sa[..., None] * (kk_t * a_t)[:, :, None, :]
             + v_t[..., None] * k_t[:, :, None, :])
        return S, jnp.einsum('bhvk,bhk->bhv', S, r_t)

    S, ys = lax.scan(step, s0, xs)
    return jnp.swapaxes(ys, 0, 1), S


def _rwkv7_mixer(h, shift_prev, wkv0, w_in, mu, w0, w2, a0, a2, k_k, k_a, r_k, lnx_g, lnx_b, w_out):
    bt, t, _ = h.shape
    proj = h @ w_in
    sh, z = proj[..., :SHIFT_COLS], proj[..., SHIFT_COLS:]
    prev = jnp.concatenate([shift_prev[:, None, :].astype(sh.dtype), sh[:, :-1]], axis=1)
    xm = sh + (prev - sh) * mu
    e = MIX_WIDTH
    r, k, v = xm[..., :e], xm[..., e:2 * e], xm[..., 2 * e:3 * e]
    wd = xm[..., 3 * e:3 * e + LORA_W]
    ad = xm[..., 3 * e + LORA_W:]
    wf = (w0 + jnp.tanh(wd) @ w2).astype(jnp.float32)
    w_log = -jax.nn.softplus(-wf) - 0.5
    decay = jnp.exp(-jnp.exp(w_log))
    a = jax.nn.sigmoid((a0 + ad @ a2).astype(jnp.float32))

    hs = lambda u: u.astype(jnp.float32).reshape(bt, t, RW_HEADS, RW_HEAD)
    hp = lambda p: p.astype(jnp.float32).reshape(RW_HEADS, RW_HEAD)
    r, k, v, a, decay = hs(r), hs(k), hs(v), hs(a), hs(decay)
    kk = k * hp(k_k)
    kk = kk / jnp.maximum(jnp.sqrt(jnp.sum(kk * kk, axis=-1, keepdims=True)), 1e-12)
    k = k * (1.0 + (a - 1.0) * hp(k_a))

    y, s_fin = _wkv7_scan(r, decay, k, v, kk, a, wkv0.astype(jnp.float32))
    mean = jnp.mean(y, axis=-1, keepdims=True)
    var = jnp.mean(jnp.square(y - mean), axis=-1, keepdims=True)
    yn = ((y - mean) * lax.rsqrt(var + GN_EPS)).reshape(bt, t, e)
    yn = yn * lnx_g.astype(jnp.float32) + lnx_b.astype(jnp.float32)
    bonus = jnp.sum(r * k * r_k.astype(jnp.float32), axis=-1, keepdims=True) * v
    o = (yn + bonus.reshape(bt, t, e)).astype(h.dtype) * jax.nn.silu(z)
    return o @ w_out, sh[:, -1], s_fin.astype(wkv0.dtype)


def _gmlp_mixer(h, w_in, v_g, v_b, ws, bs, w_out):
    bt, t, _ = h.shape
    e = MIX_WIDTH
    proj = h @ w_in
    u = jax.nn.gelu(proj[..., :e])
    v = jax.nn.gelu(proj[..., e:2 * e])
    z = proj[..., 2 * e:]
    vf = v.astype(jnp.float32)
    vm_ = jnp.mean(vf, axis=-1, keepdims=True)
    vv = jnp.mean(jnp.square(vf - vm_), axis=-1, keepdims=True)
    v = ((vf - vm_) * lax.rsqrt(vv + LN_EPS)).astype(h.dtype) * v_g + v_b
    n_chunks = -(-t // CHUNK)
    pad = n_chunks * CHUNK - t
    vp = jnp.pad(v, ((0, 0), (0, pad), (0, 0))).reshape(bt, n_chunks, CHUNK, GM_GROUPS, GM_GROUP_DIM)
    mask = jnp.tril(jnp.ones((CHUNK, CHUNK), dtype=bool))
    wm = jnp.where(mask[None], ws, jnp.zeros_like(ws))
    mixed = jnp.einsum('gts,bcsgd->bctgd', wm, vp) + bs.T[None, None, :, :, None]
    mixed = mixed.reshape(bt, n_chunks * CHUNK, e)[:, :t]
    o = u * mixed * jax.nn.silu(z)
    return o @ w_out, v


def setup_inputs(seed: int = 0) -> dict:
    key = jax.random.key(seed)
    ks = jax.random.split(key, 26)
    f32 = jnp.float32
    nrm = lambda k, shape, s: jax.random.normal(k, shape, f32) * s
    e = MIX_WIDTH
    return {
        "x_prompt": nrm(ks[0], (BATCH, SEQ, D_MODEL), 1.0),
        "x_sample": nrm(ks[1], (DEC_BATCH, DEC_SEQ, D_MODEL), 1.0),
        "state_shift": nrm(ks[2], (N_RWKV, DEC_BATCH, SHIFT_COLS), 1.0),
        "state_wkv": nrm(ks[3], (N_RWKV, DEC_BATCH, RW_HEADS, RW_HEAD, RW_HEAD), 0.3),
        "norm_g": 1.0 + nrm(ks[4], (DEPTH, D_MODEL), 0.05),
        "norm_f": 1.0 + nrm(ks[5], (D_MODEL,), 0.05),
        "rw_in": nrm(ks[6], (N_RWKV, D_MODEL, RW_IN_COLS), D_MODEL ** -0.5),
        "rw_mu": jax.random.uniform(ks[7], (N_RWKV, SHIFT_COLS), f32),
        "rw_w0": jax.random.uniform(ks[8], (N_RWKV, e), f32, -6.0, 1.0),
        "rw_w2": nrm(ks[9], (N_RWKV, LORA_W, e), 0.1),
        "rw_a0": nrm(ks[10], (N_RWKV, e), 0.5),
        "rw_a2": nrm(ks[11], (N_RWKV, LORA_A, e), 0.1),
        "rw_kk": 0.85 + nrm(ks[12], (N_RWKV, e), 0.05),
        "rw_ka": 1.0 + nrm(ks[13], (N_RWKV, e), 0.05),
        "rw_rk": nrm(ks[14], (N_RWKV, RW_HEADS, RW_HEAD), 0.1),
        "rw_lnx_g": 1.0 + nrm(ks[15], (N_RWKV, e), 0.05),
        "rw_lnx_b": nrm(ks[16], (N_RWKV, e), 0.05),
        "rw_out": nrm(ks[17], (N_RWKV, e, D_MODEL), e ** -0.5),
        "gm_in": nrm(ks[18], (N_GMLP, D_MODEL, GM_IN_COLS), D_MODEL ** -0.5),
        "gm_vg": 1.0 + nrm(ks[19], (N_GMLP, e), 0.05),
        "gm_vb": nrm(ks[20], (N_GMLP, e), 0.05),
        "gm_ws": nrm(ks[21], (N_GMLP, GM_GROUPS, CHUNK, CHUNK), CHUNK ** -0.5),
        "gm_bs": 1.0 + nrm(ks[22], (N_GMLP, GM_GROUPS, CHUNK), 0.1),
        "gm_out": nrm(ks[23], (N_GMLP, e, D_MODEL), e ** -0.5),
    }


def reference(x_prompt, x_sample, state_shift, state_wkv, norm_g, norm_f,
              rw_in, rw_mu, rw_w0, rw_w2, rw_a0, rw_a2, rw_kk, rw_ka, rw_rk, rw_lnx_g, rw_lnx_b, rw_out,
              gm_in, gm_vg, gm_vb, gm_ws, gm_bs, gm_out):
    xp, xs = x_prompt, x_sample
    p_shift, p_wkv, s_shift, s_wkv, s_v = [], [], [], [], []
    for i in range(DEPTH):
        j = i // 2
        hp = _rmsnorm(xp, norm_g[i])
        hs = _rmsnorm(xs, norm_g[i])
        if i % 2 == 0:
            prm = (rw_in[j], rw_mu[j], rw_w0[j], rw_w2[j], rw_a0[j], rw_a2[j], rw_kk[j], rw_ka[j],
                   rw_rk[j], rw_lnx_g[j], rw_lnx_b[j], rw_out[j])
            sh0 = jnp.zeros((xp.shape[0], SHIFT_COLS), xp.dtype)
            wkv0 = jnp.zeros((xp.shape[0], RW_HEADS, RW_HEAD, RW_HEAD), state_wkv.dtype)
            op, shp, wkp = _rwkv7_mixer(hp, sh0, wkv0, *prm)
            os_, shs, wks = _rwkv7_mixer(hs, state_shift[j], state_wkv[j], *prm)
            p_shift.append(shp); p_wkv.append(wkp); s_shift.append(shs); s_wkv.append(wks)
        else:
            prm = (gm_in[j], gm_vg[j], gm_vb[j], gm_ws[j], gm_bs[j], gm_out[j])
            op, _ = _gmlp_mixer(hp, *prm)
            os_, vs = _gmlp_mixer(hs, *prm)
            s_v.append(vs)
        xp = xp + op
        xs = xs + os_
    y_prompt = _rmsnorm(xp, norm_f)
    y_sample = _rmsnorm(xs, norm_f)
    prompt_shift = jnp.stack(p_shift)
    prompt_wkv = jnp.stack(p_wkv)
    sample_shift = jnp.stack(s_shift)
    sample_wkv = jnp.stack(s_wkv)
    sample_v = jnp.stack(s_v)
    return (y_prompt, y_sample, prompt_shift, prompt_wkv, sample_shift, sample_wkv, sample_v)
```

```python
import contextlib
import numpy as np
import concourse.bass as bass
import concourse.mybir as mybir
from concourse.bass_utils import run_bass_kernel_spmd

F32 = mybir.dt.float32
BF16 = mybir.dt.bfloat16
AF = mybir.ActivationFunctionType
ALU = mybir.AluOpType
AX = mybir.AxisListType

NCORES = 8
D = 1024
E = 2048
SHIFT = 6272
C_LW = -0.5 * float(np.exp(-0.5))

O_ID = 0
O_BO = 128
O_M2P = 256
O_MLP = 384
O_M2S = 448
O_MLS = 576
O_RSTP = 640
O_RSTS = 1152
O_SEQ = 1216
O_ROW = 2240
O_TRIU = 2256
NCST = 2384


class Unit:
    __slots__ = ("w", "rs")

    def __init__(self):
        self.w = None
        self.rs = []


class Op:
    __slots__ = ("eng", "fn", "deps", "sig", "cnt", "stream", "scnt", "idx", "extra", "sneed")


class Sched:
    ENGS = ("pe", "act", "dve", "pool", "sp")

    def __init__(self, nc):
        self.nc = nc
        self.ops = {e: [] for e in self.ENGS}
        self.units = {}
        self.streams = {}
        self.all_ops = []

    def U(self, key):
        u = self.units.get(key)
        if u is None:
            u = self.units[key] = Unit()
        return u

    maxops = None
    buf = None

    def op(self, eng, fn, reads=(), writes=(), stream=None):
        if self.buf is not None:
            self.buf.append((eng, fn, tuple(reads), tuple(writes), stream))
            return None
        if self.maxops is not None and len(self.all_ops) >= self.maxops:
            return None
        o = Op()
        o.eng = eng
        o.fn = fn
        o.sig = False
        o.cnt = None
        o.stream = stream
        o.scnt = None
        o.extra = None
        deps = set()
        rkeys = set(reads)
        for k in reads:
            u = self.U(k)
            if u.w is not None:
                deps.add(u.w)
            u.rs.append(o)
        for k in writes:
            u = self.U(k)
            if u.w is not None and (k in rkeys or u.w.stream is not None or stream is not None or u.w.eng != eng):
                deps.add(u.w)
            for r in u.rs:
                if r is not o and (r.stream is not None or stream is not None or r.eng != eng):
                    deps.add(r)
            u.w = o
            u.rs = []
        if stream is not None:
            s = self.streams.setdefault(stream, [0])
            s[0] += 1
            o.scnt = s[0]
        dl = []
        sneed = {}
        for d in deps:
            if d.stream is None and d.eng == eng and eng == "pe":
                continue
            if d.stream is not None:
                sneed[d.stream] = self.streams[d.stream][0] - (1 if d.stream == stream else 0)
            dl.append(d)
        o.deps = dl
        o.sneed = sneed
        o.idx = len(self.all_ops)
        self.all_ops.append(o)
        self.ops[eng].append(o)
        return o

    def fence(self):
        if self.maxops is not None and len(self.all_ops) >= self.maxops:
            return
        last = {}
        for e in self.ENGS:
            for o in reversed(self.ops[e]):
                if o.stream is None and o.fn is not None:
                    last[e] = o
                    break
        snap = {s: v[0] for s, v in self.streams.items()}
        for e in self.ENGS:
            o = Op()
            o.eng = e
            o.fn = None
            o.sig = False
            o.cnt = None
            o.stream = None
            o.scnt = None
            o.deps = [d for k, d in last.items() if not (k == e and e == "pe")]
            o.extra = snap
            o.sneed = None
            o.idx = len(self.all_ops)
            self.all_ops.append(o)
            self.ops[e].append(o)

    def emit(self, final_wait_streams=()):
        nc = self.nc
        for o in self.all_ops:
            for d in o.deps:
                if d.stream is None:
                    d.sig = True
        cnts = {e: 0 for e in self.ENGS}
        for o in self.all_ops:
            if o.stream is None and o.sig:
                cnts[o.eng] += 1
                o.cnt = cnts[o.eng]
        self.sigcnts = cnts
        with contextlib.ExitStack() as st:
            esem = {e: st.enter_context(nc.semaphore("s_" + e)) for e in self.ENGS}
            ssem = {s: st.enter_context(nc.semaphore("d_" + s)) for s in self.streams}
            block = st.enter_context(nc.Block())
            sched = self

            def run(engname, e):
                seen = {}
                for o in sched.ops[engname]:
                    need = {}
                    for d in o.deps:
                        if d.stream is not None:
                            key = ("s", d.stream)
                            v = max(d.scnt, (o.sneed or {}).get(d.stream, 0)) * 16
                        else:
                            key = ("e", d.eng)
                            v = d.cnt
                        if v > need.get(key, 0):
                            need[key] = v
                    if o.extra:
                        for s, c in o.extra.items():
                            if c * 16 > need.get(("s", s), 0):
                                need[("s", s)] = c * 16
                    for key, v in need.items():
                        if seen.get(key, 0) >= v:
                            continue
                        seen[key] = v
                        sem = ssem[key[1]] if key[0] == "s" else esem[key[1]]
                        e.wait_ge(sem, v)
                    if o.fn is None:
                        continue
                    ins = o.fn(e)
                    if o.stream is not None:
                        ins.then_inc(ssem[o.stream], 16)
                    elif o.sig:
                        ins.then_inc(esem[engname], 1)
                if engname == "sp":
                    for s in final_wait_streams:
                        e.wait_ge(ssem[s], sched.streams[s][0] * 16)

            @block.tensor
            def _(e):
                run("pe", e)

            @block.scalar
            def _(e):
                run("act", e)

            @block.vector
            def _(e):
                run("dve", e)

            @block.gpsimd
            def _(e):
                run("pool", e)

            @block.sync
            def _(e):
                run("sp", e)


def build_nc(tiles=("P0", "P1", "P2", "P3", "S")):
    nc = bass.Bass("TRN2", target_bir_lowering=False)
    S = Sched(nc)
    S.maxops = _DEBUG.get("maxops")

    def din(name, shape):
        return nc.dram_tensor(name, list(shape), F32, kind="ExternalInput").ap()

    def dout(name, shape):
        return nc.dram_tensor(name, list(shape), F32, kind="ExternalOutput").ap()

    xp = din("xp", [2048, D])
    xs = din("xs", [64, D])
    sshift = din("sshift", [16, SHIFT])
    swkv = din("swkv", [16, 32, 64, 64])
    gn = din("gn", [3, D])
    win0 = din("win0", [D, 8320])
    pp = din("pp", [128, 130])
    w2aug = din("w2aug", [65, E])
    a2aug = din("a2aug", [65, E])
    wout0 = din("wout0", [E, D])
    win1 = din("win1", [D, 6144])
    vgb = din("vgb", [2, E])
    wsd = din("ws", [8, 128, 128])
    bsd = din("bs", [1, 1024])
    wout1 = din("wout1", [E, D])
    cstd = din("cst", [128, NCST])

    yp = dout("yp", [2048, D])
    ys = dout("ys", [64, D])
    pshift = dout("pshift", [128, 50])
    pwkv = dout("pwkv", [32, 64, 64])
    sshift_o = dout("sshift_o", [16, SHIFT])
    swkv_o = dout("swkv_o", [16, 32, 64, 64])
    sv = dout("sv", [64, E])

    def sb(name, shape, dt=F32):
        return nc.alloc_sbuf_tensor("sb_" + name, list(shape), dt)

    def psum(name, shape, dt=F32):
        return nc.alloc_psum_tensor("ps_" + name, list(shape), dt)

    cst = sb("cst", [128, NCST])
    identb = sb("identb", [128, 128], BF16)
    bones = sb("bones", [128, 128], BF16)
    ppt = sb("ppt", [128, 130])
    pc = sb("pc", [128, 80])
    bdrk = sb("bdrk", [128, 16, 128], BF16)
    w2b = sb("w2b", [65, E], BF16)
    a2b = sb("a2b", [65, E], BF16)
    gbc = sb("gbc", [128, 3, D])
    vgbc = sb("vgbc", [128, 2, E])
    wmT = sb("wmT", [128, 8, 128], BF16)
    bdtb = sb("bdtb", [64, 8, 64], BF16)
    bsP = sb("bsP", [1, 1024], BF16)
    bsS = sb("bsS", [1, 8, 64], BF16)
    onesr = sb("onesr", [1, 128], BF16)
    carryP = sb("carryP", [128, 50, 1])
    SPb = sb("SPb", [128, 2, 16, 64], BF16)
    xres = sb("xres", [128, 4, D])
    hb = sb("hb", [128, D], BF16)
    hT = sb("hT", [128, 8, 512], BF16)
    NWG = 2
    wg = [sb("wg%d" % i, [128, 8, 512], BF16) for i in range(NWG)]
    ofm = sb("ofm", [128, 16, 512], BF16)
    stat = sb("stat", [128, 64])
    epsk = sb("epsk", [128, 2])
    twd = sb("twd", [65, 512], BF16)
    adg = sb("adg", [65, 512], BF16)
    ARENA = 20608
    arena = sb("arena", [128, ARENA])

    class Carve:
        def __init__(self, off=0):
            self.off = off

        def f32(self, n, rows=128):
            a = arena[0:rows, self.off:self.off + n]
            self.off += n
            assert self.off <= ARENA, self.off
            return a

        def bf(self, n):
            m = (n + 1) // 2
            a = arena[:, self.off:self.off + m].bitcast(BF16)
            self.off += m
            assert self.off <= ARENA, self.off
            return a

    class NS:
        pass

    sinT = arena[:, ARENA - 1600:ARENA - 800].rearrange("p (c b) -> p c b", b=16)
    soutT = arena[:, ARENA - 800:ARENA].rearrange("p (c b) -> p c b", b=16)

    def carve_l0(samp):
        N = 64 if samp else 512
        c0 = Carve()
        B = NS()
        B.sh = c0.f32(3 * 520)
        B.xm = c0.f32(3 * 512)
        lo = c0.off
        for nm in ("lw", "Lc", "av", "sq", "kkn", "k2", "bbv", "Pinv", "Pm", "yTM", "Wf"):
            setattr(B, nm, c0.f32(N))
        B.gz = [c0.f32(N), c0.f32(N)]
        B.bon = [c0.f32(N), c0.f32(N)]
        B.Pt = [c0.f32(N), c0.f32(N)]
        B.tmpS = c0.f32(1024 if samp else 512)
        if samp:
            B.Ssf = c0.f32(1024)
            B.Snat = c0.f32(1024)
        B.rkb = c0.bf(N)
        B.AR = [c0.bf(2 * N), c0.bf(2 * N)]
        B.BK = [c0.bf(2 * N), c0.bf(2 * N)]
        B.Vb = [c0.bf(N), c0.bf(N)]
        B.BT = [c0.bf(N), c0.bf(N)]
        B.KT = [c0.bf(N), c0.bf(N)]
        B.VT = [c0.bf(N), c0.bf(N)]
        B.XA = [c0.bf(2 * N), c0.bf(2 * N)]
        B.XB = [c0.bf(2 * N), c0.bf(2 * N)]
        B.Yb = [c0.bf(N), c0.bf(N)]
        B.Zb = [c0.bf(N), c0.bf(N)]
        B.Wb = [c0.bf(N), c0.bf(N)]
        B.Gb = c0.bf(64)
        B.Ub = c0.bf(64)
        B.ynb = c0.bf(N)
        if samp:
            B.Ssb = c0.bf(1024)
            B.ATm = c0.bf(1024)
            B.RTm = c0.bf(1024)
            B.Um = c0.bf(1024)
            B.Vm = c0.bf(1024)
        cl = Carve(lo)
        B.shl = cl.f32(1040, rows=64).rearrange("p (i n) -> p i n", n=520)
        B.xml = cl.f32(1024, rows=64).rearrange("p (i n) -> p i n", n=512)
        B.dl = cl.f32(1024, rows=64).rearrange("p (i n) -> p i n", n=512)
        return B

    c1 = Carve()
    gvs = c1.f32(2048)
    gsc = c1.f32(512)
    gsc2 = c1.f32(512)
    gu2 = [c1.f32(512), c1.f32(512)]
    zt12 = [c1.f32(512), c1.f32(512)]
    gz12 = [c1.f32(512), c1.f32(512)]
    o1t2 = [c1.f32(512), c1.f32(512)]
    gvb = c1.bf(4 * 2048)
    t32 = c1.f32(2048)
    vnb = c1.bf(4 * 2048)
    c2 = Carve()
    sTM = c2.f32(SHIFT)
    c3_ = Carve()
    wmTs = c3_.f32(1024).rearrange("p (g t) -> p g t", t=128)
    wnat = c3_.f32(1024).rearrange("p (g t) -> p g t", t=128)
    bdt = c3_.f32(512, rows=64).rearrange("p (g t) -> p g t", t=64)
    bsf = c3_.f32(1024, rows=1)
    c4 = Carve(SHIFT)
    SnatE = c4.f32(1024)
    SPf = c4.f32(1024)

    pj = [psum("pjA", [128, 512]), psum("pjB", [128, 512])]
    pm = psum("pm", [128, 512])
    ptrf = psum("ptr", [128, 512])
    ptr = ptrf[:, :].bitcast(BF16)
    pM = psum("pM", [128, 512])
    pY = psum("pY", [128, 512])
    pZ = psum("pZ", [128, 512])
    pch = psum("pch", [128, 512])

    ALLB = [(pj[0], ("pj", 0)), (pj[1], ("pj", 1)), (pm, "pm"), (pM, "pM"), (pY, "pY"), (pZ, "pZ"), (pch, "pch")]
    brot = [0]

    def nextbank():
        b_ = ALLB[brot[0] % len(ALLB)]
        brot[0] += 1
        return b_

    wcnt = [0]

    def load_group(src_ap, ncols):
        slot = wcnt[0] % NWG
        wcnt[0] += 1
        t = wg[slot]
        S.op("pool", lambda e: e.dma_start(out=t[:, :, 0:ncols], in_=src_ap.rearrange("(k p) c -> p k c", p=128)),
             writes=[("wg", slot)], stream="wg%d" % slot)
        return slot

    def cval(off, n, rows=128):
        return cst[0:rows, off:off + n]

    marks = {}

    def mark(name):
        marks.setdefault(name, len(S.all_ops))

    S.op("sp", lambda e: e.dma_start(out=cst[:, :], in_=cstd), writes=["cst"], stream="c0")
    S.op("sp", lambda e: e.dma_start(out=ppt[:, :], in_=pp), writes=["ppt"], stream="c1")
    S.op("sp", lambda e: e.dma_start(out=gbc[:, :, :], in_=gn.partition_broadcast(128)), writes=["gbc"], stream="c2")
    S.op("sp", lambda e: e.dma_start(out=vgbc[:, :, :], in_=vgb.partition_broadcast(128)), writes=["vgbc"], stream="c3")
    S.op("pool", lambda e: e.dma_start(out=w2b[:, :], in_=w2aug), writes=["w2b"], stream="c4")
    S.op("pool", lambda e: e.dma_start(out=a2b[:, :], in_=a2aug), writes=["a2b"], stream="c5")
    S.op("sp", lambda e: e.dma_start(out=wnat[:, :, :], in_=wsd.rearrange("g t s -> t g s")), writes=["wnat"], stream="c6")
    S.op("sp", lambda e: e.dma_start(out=bsf[:, :], in_=bsd), writes=["bsf"], stream="c7")
    S.op("dve", lambda e: e.tensor_copy(out=identb[:, :], in_=cval(O_ID, 128)), reads=["cst"], writes=["identb"])
    S.op("dve", lambda e: e.tensor_copy(out=bones[:, :], in_=cval(O_BO, 128)), reads=["cst"], writes=["bones"])
    S.op("dve", lambda e: e.tensor_scalar(out=pc[:, 0:16], in0=ppt[:, 64:80], scalar1=0.5, scalar2=None, op0=ALU.mult), reads=["ppt"], writes=["pc"])
    S.op("dve", lambda e: e.tensor_scalar(out=pc[:, 16:32], in0=ppt[:, 64:80], scalar1=-0.5, scalar2=1.0, op0=ALU.mult, op1=ALU.add), reads=["ppt"], writes=["pc"])
    S.op("dve", lambda e: e.tensor_scalar(out=pc[:, 32:64], in0=ppt[:, 80:112], scalar1=0.5, scalar2=None, op0=ALU.mult), reads=["ppt"], writes=["pc"])
    S.op("dve", lambda e: e.tensor_scalar(out=pc[:, 64:80], in0=ppt[:, 112:128], scalar1=0.5, scalar2=None, op0=ALU.mult), reads=["ppt"], writes=["pc"])
    for j in range(16):
        S.op("dve", lambda e, j=j: e.tensor_scalar(out=bdrk[:, j, :], in0=cval(O_BO, 128), scalar1=pc[:, 64 + j:65 + j], scalar2=None, op0=ALU.mult),
             reads=["cst", "pc"], writes=["bdrk"])
    S.op("pool", lambda e: e.memset(twd[:, :], 1.0), writes=["twd"])
    S.op("pool", lambda e: e.memset(epsk[:, 0:1], 1e-24), writes=["epsk"])
    S.op("pool", lambda e: e.memset(epsk[:, 1:2], 64e-5), writes=["epsk"])
    S.op("pool", lambda e: e.memset(adg[:, :], 1.0), writes=["adg"])
    S.op("pool", lambda e: e.memset(onesr[:, :], 1.0), writes=["onesr"])
    S.op("pool", lambda e: e.memset(carryP[:, :, :], 0.0), writes=["carryP"])
    for j in range(16):
        S.op("pool", lambda e, j=j: e.memset(SPb[:, 0, j, :], 0.0), writes=[("SPb", j, 0)])
    S.op("pool", lambda e: e.memset(bdt[:, :, :], 0.0), writes=["bdt"])
    for g in range(8):
        S.op("pe", lambda e, g=g: e.transpose(out=(pj[0] if g < 4 else pj[1])[:, (g % 4) * 128:(g % 4 + 1) * 128], in_=wnat[:, g, :], identity=cval(O_ID, 128)),
             reads=["wnat", "cst"], writes=[("pj", 0 if g < 4 else 1)])
    for hh in range(2):
        S.op("dve", lambda e, hh=hh: e.scalar_tensor_tensor(out=wmTs[:, hh * 4:hh * 4 + 4, :], in0=pj[hh][:, :].rearrange("p (g t) -> p g t", t=128), scalar=0.5,
                                                            in1=cval(O_TRIU, 128).unsqueeze(1).to_broadcast([128, 4, 128]), op0=ALU.mult, op1=ALU.mult),
             reads=[("pj", hh), "cst"], writes=["wmTs"])
    S.op("act", lambda e: e.activation(out=wmT[:, :, :], in_=wmTs[:, :, :], func=AF.Copy), reads=["wmTs"], writes=["wmT"])
    for b in range(16):
        S.op("sp", lambda e, b=b: e.dma_start(out=bdt[4 * b:4 * b + 4, :, 4 * b:4 * b + 4], in_=wmTs[0:4, :, 0:4]), reads=["wmTs"], writes=["bdt"], stream="c8")
    S.op("act", lambda e: e.activation(out=bdtb[:, :, :], in_=bdt[:, :, :], func=AF.Copy), reads=["bdt"], writes=["bdtb"])
    S.op("act", lambda e: e.activation(out=bsP[:, :], in_=bsf[:, :], func=AF.Copy, scale=0.5), reads=["bsf"], writes=["bsP"])
    S.op("dve", lambda e: e.tensor_copy(out=bsS[:, :, :].rearrange("p g (b t) -> p g b t", t=4),
                                        in_=bsP[:, :].rearrange("p (g t) -> p g t", t=128)[:, :, 0:4].unsqueeze(2).to_broadcast([1, 8, 16, 4])),
         reads=["bsP"], writes=["bsS"])

    S.fence()

    def rmsnorm_T(kind, nsub, RP, layer):
        for s in range(nsub):
            S.op("act", lambda e, s=s: e.activation(out=hb[0:RP, :], in_=xres[0:RP, s, :], func=AF.Square, accum_out=stat[0:RP, s:s + 1]),
                 reads=[("xres", s)], writes=["hb", ("stat", s)])
            S.op("dve", lambda e, s=s: e.tensor_scalar(out=stat[0:RP, 8 + s:9 + s], in0=stat[0:RP, s:s + 1], scalar1=1.0 / D, scalar2=1e-6, op0=ALU.mult, op1=ALU.add),
                 reads=[("stat", s)], writes=[("stat", 8 + s)])
            S.op("act", lambda e, s=s: e.activation(out=stat[0:RP, 16 + s:17 + s], in_=stat[0:RP, 8 + s:9 + s], func=AF.Sqrt),
                 reads=[("stat", 8 + s)], writes=[("stat", 16 + s)])
            S.op("dve", lambda e, s=s: e.reciprocal(out=stat[0:RP, 24 + s:25 + s], in_=stat[0:RP, 16 + s:17 + s]),
                 reads=[("stat", 16 + s)], writes=[("stat", 24 + s)])
            S.op("dve", lambda e, s=s: e.scalar_tensor_tensor(out=hb[0:RP, :], in0=xres[0:RP, s, :], scalar=stat[0:RP, 24 + s:25 + s], in1=gbc[0:RP, layer, :],
                                                              op0=ALU.mult, op1=ALU.mult),
                 reads=[("xres", s), ("stat", 24 + s), "gbc"], writes=["hb"])
            for k in range(8):
                S.op("pe", lambda e, k=k: e.transpose(out=ptr[:, k * 128:k * 128 + RP], in_=hb[0:RP, k * 128:(k + 1) * 128], identity=identb[0:RP, 0:RP]),
                     reads=["hb", "identb"], writes=["ptr"])
            S.op("act", lambda e, s=s: e.activation(out=hT[:, :, s * 128:s * 128 + RP], in_=ptr[:, :].rearrange("p (k t) -> p k t", t=128)[:, :, 0:RP], func=AF.Copy),
                 reads=["ptr"], writes=["hT"])

    def out_proj(wsrc, nsub, RP, NT):
        def body(n):
            slots = [load_group(wsrc[kh * 1024:(kh + 1) * 1024, n * 512:(n + 1) * 512], 512) for kh in range(2)]
            for s in range(nsub):
                pb, pu = nextbank()
                for kt in range(16):
                    S.op("pe", lambda e, kt=kt, s=s, pb=pb: e.matmul(pb[0:RP, :], lhsT=ofm[:, kt, s * 128:s * 128 + RP], rhs=wg[slots[kt // 8]][:, kt % 8, :],
                                                                     start=(kt == 0), stop=(kt == 15)),
                         reads=["ofm", ("wg", slots[kt // 8])], writes=[pu])
                S.op("dve", lambda e, s=s, pb=pb, n=n: e.tensor_tensor(out=xres[0:RP, s, n * 512:(n + 1) * 512], in0=pb[0:RP, :], in1=xres[0:RP, s, n * 512:(n + 1) * 512], op=ALU.add),
                     reads=[pu, ("xres", s)], writes=[("xres", s)])
        for n in range(2):
            body(n)

    def layer0(kind, nsub, RP, NT, nseq, T):
        samp = kind == "S"
        cin = sinT if samp else carryP
        cout = soutT if samp else carryP
        cin_u = "sinT" if samp else "carryP"
        cout_u = "soutT" if samp else "carryP"
        nb = nseq
        o_m2 = O_M2S if samp else O_M2P
        o_ml = O_MLS if samp else O_MLP
        o_rst = O_RSTS if samp else O_RSTP
        nch = NT // 64
        B = carve_l0(samp)
        sh, xm, lw, Lc, av, sq, kkn, k2, bbv = B.sh, B.xm, B.lw, B.Lc, B.av, B.sq, B.kkn, B.k2, B.bbv
        Pinv, Pm, yTM, Wf, tmpS, rkb = B.Pinv, B.Pm, B.yTM, B.Wf, B.tmpS, B.rkb
        Yb, Zb, Gb, Ub, ynb, shl, xml, dl_ = B.Yb, B.Zb, B.Gb, B.Ub, B.ynb, B.shl, B.xml, B.dl
        if samp:
            Ssf, Snat, Ssb, ATm, RTm, Um, Vm = B.Ssf, B.Snat, B.Ssb, B.ATm, B.RTm, B.Um, B.Vm

        def v3(ap, t):
            return ap.rearrange("p (b t) -> p b t", t=t)

        mark("l0_start")
        rmsnorm_T(kind, nsub, RP, 0)
        mark("l0_lora")
        slot = load_group(win0[:, 0:128], 128)
        for i in range(2):
            for k in range(8):
                S.op("pe", lambda e, i=i, k=k: e.matmul(pj[i][0:64, 0:NT], lhsT=wg[slot][:, k, i * 64:(i + 1) * 64], rhs=hT[:, k, 0:NT], start=(k == 0), stop=(k == 7)),
                     reads=["hT", ("wg", slot)], writes=[("pj", i)])
            shv = v3(shl[:, i, 0:nb * (T + 1)], T + 1)
            S.op("act", lambda e, i=i, shv=shv: e.activation(out=shv[:, :, 0:1], in_=cin[0:64, 48 + i, 0:nb].unsqueeze(2), func=AF.Copy), reads=[cin_u], writes=["shl"])
            S.op("act", lambda e, i=i, shv=shv: e.activation(out=shv[:, :, 1:T + 1], in_=v3(pj[i][0:64, 0:NT], T), func=AF.Copy), reads=[("pj", i)], writes=["shl"])
            S.op("act", lambda e, i=i, shv=shv: e.activation(out=cout[0:64, 48 + i, 0:nb].unsqueeze(2), in_=shv[:, :, T:T + 1], func=AF.Copy), reads=["shl"], writes=[cout_u])
            S.op("dve", lambda e, i=i, shv=shv: e.tensor_tensor(out=v3(dl_[:, i, 0:NT], T), in0=shv[:, :, 0:T], in1=shv[:, :, 1:T + 1], op=ALU.subtract), reads=["shl"], writes=["dl"])
            S.op("dve", lambda e, i=i, shv=shv: e.scalar_tensor_tensor(out=v3(xml[:, i, 0:NT], T), in0=v3(dl_[:, i, 0:NT], T), scalar=ppt[0:64, 128 + i:129 + i], in1=shv[:, :, 1:T + 1],
                                                                       op0=ALU.mult, op1=ALU.add), reads=["dl", "shl", "ppt"], writes=["xml"])
        S.op("act", lambda e: e.activation(out=twd[0:64, 0:NT], in_=xml[:, 0, 0:NT], func=AF.Tanh), reads=["xml"], writes=["twd"])
        S.op("act", lambda e: e.activation(out=adg[0:64, 0:NT], in_=xml[:, 1, 0:NT], func=AF.Copy), reads=["xml"], writes=["adg"])
        S.fence()

        def stageABC(j):
            par = j % 2
            gz, bon, Pt, AR, BK, Vb = B.gz[par], B.bon[par], B.Pt[par], B.AR[par], B.BK[par], B.Vb[par]
            BT, KT, VT, XA, XB, Wb = B.BT[par], B.KT[par], B.VT[par], B.XA[par], B.XB[par], B.Wb[par]
            XAv = XA[:, 0:nch * 128].rearrange("p (c n) -> p c n", n=128)
            XBv = XB[:, 0:nch * 128].rearrange("p (c n) -> p c n", n=128)
            ARv = AR[:, 0:nch * 128].rearrange("p (c two t) -> p c two t", two=2, t=64)
            BKv = BK[:, 0:nch * 128].rearrange("p (c two t) -> p c two t", two=2, t=64)

            def c3(ap):
                return ap.rearrange("p (c t) -> p c t", t=64)

            mark("l0_j%d" % j)
            if j + 1 < 16:
                gslots[j + 1] = load_group(win0[:, 128 + (j + 1) * 512:128 + (j + 2) * 512], 512)
            slot = gslots[j]
            jsl = slice(j, j + 1)
            for i in range(4):
                pb = pj[i % 2]
                for k in range(8):
                    S.op("pe", lambda e, i=i, k=k, pb=pb: e.matmul(pb[:, 0:NT], lhsT=wg[slot][:, k, i * 128:(i + 1) * 128], rhs=hT[:, k, 0:NT], start=(k == 0), stop=(k == 7)),
                         reads=["hT", ("wg", slot)], writes=[("pj", i % 2)])
                if i < 3:
                    cidx = i * 16 + j
                    shv = v3(sh[:, i * 520:i * 520 + nb * (T + 1)], T + 1)
                    S.op("act", lambda e, shv=shv, cidx=cidx: e.activation(out=shv[:, :, 0:1], in_=cin[:, cidx, 0:nb].unsqueeze(2), func=AF.Copy), reads=[cin_u], writes=[("sh", i)])
                    S.op("act", lambda e, shv=shv, pb=pb: e.activation(out=shv[:, :, 1:T + 1], in_=v3(pb[:, 0:NT], T), func=AF.Copy), reads=[("pj", i % 2)], writes=[("sh", i)])
                    S.op("act", lambda e, shv=shv, cidx=cidx: e.activation(out=cout[:, cidx, 0:nb].unsqueeze(2), in_=shv[:, :, T:T + 1], func=AF.Copy), reads=[("sh", i)], writes=[cout_u])
                    S.op("dve", lambda e, shv=shv: e.tensor_tensor(out=v3(Pm[:, 0:NT], T), in0=shv[:, :, 0:T], in1=shv[:, :, 1:T + 1], op=ALU.subtract), reads=[("sh", i)], writes=["Pm"])
                    S.op("dve", lambda e, shv=shv, i=i: e.scalar_tensor_tensor(out=v3(xm[:, i * 512:i * 512 + NT], T), in0=v3(Pm[:, 0:NT], T), scalar=ppt[:, i * 16 + j:i * 16 + j + 1],
                                                                               in1=shv[:, :, 1:T + 1], op0=ALU.mult, op1=ALU.add), reads=["Pm", ("sh", i), "ppt"], writes=[("xm", i)])
                else:
                    S.op("act", lambda e, pb=pb: e.activation(out=Lc[:, 0:NT], in_=pb[:, 0:NT], func=AF.Tanh, scale=0.5), reads=[("pj", i % 2)], writes=["Lc"])
                    S.op("dve", lambda e, pb=pb: e.scalar_tensor_tensor(out=gz[:, 0:NT], in0=Lc[:, 0:NT], scalar=1.0, in1=pb[:, 0:NT], op0=ALU.add, op1=ALU.mult),
                         reads=["Lc", ("pj", i % 2)], writes=[("gz", par)])
            mark("l0_prep%d" % j)
            r_ = xm[:, 0:NT]
            k_ = xm[:, 512:512 + NT]
            v_ = xm[:, 1024:1024 + NT]
            S.op("pe", lambda e: e.matmul(pm[:, 0:NT], lhsT=w2b[:, j * 128:(j + 1) * 128], rhs=twd[:, 0:NT], start=True, stop=True), reads=["w2b", "twd"], writes=["pm"])
            S.op("act", lambda e: e.activation(out=lw[:, 0:NT], in_=pm[:, 0:NT], func=AF.Tanh, scale=0.5), reads=["pm"], writes=["lw"])
            S.op("dve", lambda e: e.tensor_scalar(out=lw[:, 0:NT], in0=lw[:, 0:NT], scalar1=C_LW, scalar2=C_LW, op0=ALU.mult, op1=ALU.add), reads=["lw"], writes=["lw"])
            S.op("pe", lambda e: e.matmul(pm[:, 0:NT], lhsT=a2b[:, j * 128:(j + 1) * 128], rhs=adg[:, 0:NT], start=True, stop=True), reads=["a2b", "adg"], writes=["pm"])
            S.op("act", lambda e: e.activation(out=av[:, 0:NT], in_=pm[:, 0:NT], func=AF.Tanh, scale=0.5), reads=["pm"], writes=["av"])
            S.op("dve", lambda e: e.tensor_scalar(out=k2[:, 0:NT], in0=av[:, 0:NT], scalar1=pc[:, j:j + 1], scalar2=pc[:, 16 + j:17 + j], op0=ALU.mult, op1=ALU.add),
                 reads=["av", "pc"], writes=["k2"])
            S.op("dve", lambda e: e.tensor_tensor(out=k2[:, 0:NT], in0=k2[:, 0:NT], in1=k_, op=ALU.mult), reads=["k2", ("xm", 1)], writes=["k2"])
            S.op("dve", lambda e: e.tensor_scalar(out=av[:, 0:NT], in0=av[:, 0:NT], scalar1=0.5, scalar2=0.5, op0=ALU.mult, op1=ALU.add), reads=["av"], writes=["av"])
            S.op("act", lambda e: e.activation(out=sq[:, 0:NT], in_=k_, func=AF.Square, scale=ppt[:, 48 + j:49 + j]), reads=[("xm", 1), "ppt"], writes=["sq"])
            S.op("pe", lambda e: e.matmul(pm[:, 0:NT], lhsT=cval(O_BO, 128), rhs=sq[:, 0:NT], start=True, stop=True), reads=["cst", "sq"], writes=["pm"])
            S.op("act", lambda e: e.activation(out=sq[:, 0:NT], in_=pm[:, 0:NT], func=AF.Ln, bias=epsk[:, 0:1]), reads=["pm", "epsk"], writes=["sq"])
            S.op("act", lambda e: e.activation(out=sq[:, 0:NT], in_=sq[:, 0:NT], func=AF.Exp, scale=-0.5), reads=["sq"], writes=["sq"])
            S.op("dve", lambda e: e.scalar_tensor_tensor(out=kkn[:, 0:NT], in0=k_, scalar=ppt[:, 48 + j:49 + j], in1=sq[:, 0:NT], op0=ALU.mult, op1=ALU.mult),
                 reads=[("xm", 1), "ppt", "sq"], writes=["kkn"])
            S.op("dve", lambda e: e.tensor_tensor(out=bbv[:, 0:NT], in0=kkn[:, 0:NT], in1=av[:, 0:NT], op=ALU.mult), reads=["kkn", "av"], writes=["bbv"])
            S.op("dve", lambda e: e.tensor_tensor(out=rkb[:, 0:NT], in0=r_, in1=k2[:, 0:NT], op=ALU.mult), reads=[("xm", 0), "k2"], writes=["rkb"])
            S.op("pe", lambda e: e.matmul(pm[:, 0:NT], lhsT=bdrk[:, j, :], rhs=rkb[:, 0:NT], start=True, stop=True), reads=["bdrk", "rkb"], writes=["pm"])
            S.op("dve", lambda e: e.tensor_tensor(out=bon[:, 0:NT], in0=pm[:, 0:NT], in1=v_, op=ALU.mult), reads=["pm", ("xm", 2)], writes=[("bon", par)])
            S.op("dve", lambda e: e.tensor_tensor_scan(out=Lc[:, 0:NT], data0=cval(o_rst, NT), data1=lw[:, 0:NT], initial=0.0, op0=ALU.mult, op1=ALU.add),
                 reads=["cst", "lw"], writes=["Lc"])
            S.op("act", lambda e: e.activation(out=Pt[:, 0:NT], in_=Lc[:, 0:NT], func=AF.Exp), reads=["Lc"], writes=[("Pt", par)])
            S.op("act", lambda e: e.activation(out=Pinv[:, 0:NT], in_=Lc[:, 0:NT], func=AF.Exp, scale=-1.0), reads=["Lc"], writes=["Pinv"])
            S.op("dve", lambda e: e.tensor_tensor(out=Pm[:, 0:NT], in0=Lc[:, 0:NT], in1=lw[:, 0:NT], op=ALU.subtract), reads=["Lc", "lw"], writes=["Pm"])
            S.op("act", lambda e: e.activation(out=Pm[:, 0:NT], in_=Pm[:, 0:NT], func=AF.Exp), reads=["Pm"], writes=["Pm"])
            S.op("dve", lambda e: e.scalar_tensor_tensor(out=ARv[:, :, 0, :], in0=c3(kkn[:, 0:NT]), scalar=-1.0, in1=c3(Pm[:, 0:NT]), op0=ALU.mult, op1=ALU.mult),
                 reads=["kkn", "Pm"], writes=[("AR", par)])
            S.op("dve", lambda e: e.tensor_tensor(out=ARv[:, :, 1, :], in0=c3(r_), in1=c3(Pt[:, 0:NT]), op=ALU.mult), reads=[("xm", 0), ("Pt", par)], writes=[("AR", par)])
            S.op("dve", lambda e: e.tensor_tensor(out=BKv[:, :, 0, :], in0=c3(bbv[:, 0:NT]), in1=c3(Pinv[:, 0:NT]), op=ALU.mult), reads=["bbv", "Pinv"], writes=[("BK", par)])
            S.op("dve", lambda e: e.tensor_tensor(out=BKv[:, :, 1, :], in0=c3(k2[:, 0:NT]), in1=c3(Pinv[:, 0:NT]), op=ALU.mult), reads=["k2", "Pinv"], writes=[("BK", par)])
            S.op("act", lambda e: e.activation(out=Vb[:, 0:NT], in_=v_, func=AF.Copy), reads=[("xm", 2)], writes=[("Vb", par)])
            mark("l0_tm%d" % j)
            tmb = ((pm, "pm"), (pY, "pY"), (pZ, "pZ"))
            tmg = ((lambda c: BKv[:, c, 0, :], BT, ("BT", par)), (lambda c: BKv[:, c, 1, :], KT, ("KT", par)), (lambda c: Vb[:, c * 64:(c + 1) * 64], VT, ("VT", par)))
            for gi, (src, dst, uu) in enumerate(tmg):
                pb, pu = tmb[gi]
                for c in range(nch):
                    for h in range(2):
                        hs = slice(64 * h, 64 * h + 64)
                        S.op("pe", lambda e, c=c, hs=hs, h=h, src=src, pb=pb: e.matmul(pb[hs, c * 64:(c + 1) * 64], lhsT=src(c)[hs, :], rhs=identb[hs, hs], start=True, stop=True,
                                                                                    tile_position=(64 * h, 64 * h)),
                             reads=[("BK", par), ("Vb", par), "identb"], writes=[pu])
            for gi, (src, dst, uu) in enumerate(tmg):
                pb, pu = tmb[gi]
                S.op("act", lambda e, dst=dst, pb=pb: e.activation(out=dst[:, 0:NT], in_=pb[:, 0:NT], func=AF.Copy), reads=[pu], writes=[uu])
            mark("l0_mat%d" % j)
            xgb = [(pm, "pm"), (pY, "pY"), (pZ, "pZ"), (pm, "pm")]
            xgs = []
            for (which, dstv, uu) in ((0, XAv, ("XA", par)), (1, XBv, ("XB", par))):
                for c0_ in range(0, nch, 4):
                    xgs.append((which, dstv, uu, c0_, min(4, nch - c0_)))
            def xg_ev(gi):
                which, dstv, uu, c0_, cn = xgs[gi]
                pb, pu = xgb[gi]
                S.op("dve", lambda e, c0_=c0_, cn=cn, dstv=dstv, pb=pb: e.tensor_tensor(out=dstv[:, c0_:c0_ + cn, :], in0=pb[:, 0:cn * 128].rearrange("p (c n) -> p c n", n=128),
                                                                                       in1=cval(o_m2, 128).unsqueeze(1).to_broadcast([128, cn, 128]), op=ALU.mult),
                     reads=[pu, "cst"], writes=[uu])

            for gi, (which, dstv, uu, c0_, cn) in enumerate(xgs):
                pb, pu = xgb[gi]
                if gi == 3:
                    xg_ev(0)
                for c in range(c0_, c0_ + cn):
                    for h in range(2):
                        hs = slice(64 * h, 64 * h + 64)
                        S.op("pe", lambda e, c=c, hs=hs, h=h, which=which, c0_=c0_, pb=pb: e.matmul(pb[hs, (c - c0_) * 128:(c - c0_ + 1) * 128], lhsT=BKv[hs, c, which, :],
                                                                                                    rhs=AR[hs, c * 128:(c + 1) * 128], start=True, stop=True, tile_position=(64 * h, 64 * h)),
                             reads=[("BK", par), ("AR", par)], writes=[pu])
            for gi in range(len(xgs)):
                if gi == 0 and len(xgs) > 3:
                    continue
                xg_ev(gi)
            for c in range(nch):
                for h in range(2):
                    hs = slice(64 * h, 64 * h + 64)
                    S.op("pe", lambda e, c=c, hs=hs, h=h: e.matmul(pZ[hs, c * 64:(c + 1) * 64], lhsT=ARv[hs, c, 0, :], rhs=BKv[hs, c, 0, :], start=True, stop=True,
                                                                   tile_position=(64 * h, 64 * h)), reads=[("AR", par), ("BK", par)], writes=["pZ"])
            S.op("dve", lambda e: e.tensor_tensor(out=c3(Zb[0][:, 0:NT]), in0=c3(pZ[:, 0:NT]), in1=cval(o_ml, 64).unsqueeze(1).to_broadcast([128, nch, 64]), op=ALU.mult),
                 reads=["pZ", "cst"], writes=[("Zb", 0)])
            S.op("act", lambda e: e.activation(out=c3(Yb[0][:, 0:NT]), in_=XAv[:, :, 0:64], func=AF.Copy), reads=[("XA", par)], writes=[("Yb", 0)])
            S.op("dve", lambda e: e.tensor_tensor(out=c3(Wf[:, 0:NT]), in0=XAv[:, :, 0:64], in1=cval(O_ID, 64).unsqueeze(1).to_broadcast([128, nch, 64]), op=ALU.add),
                 reads=[("XA", par), "cst"], writes=["Wf"])
            S.op("dve", lambda e: e.tensor_tensor(out=c3(Wf[64:128, 0:NT]), in0=XAv[64:128, :, 0:64], in1=cst[64:128, O_ID + 64:O_ID + 128].unsqueeze(1).to_broadcast([64, nch, 64]), op=ALU.add),
                 reads=[("XA", par), "cst", "Wf"], writes=["Wf"])
            S.op("act", lambda e: e.activation(out=Wb[:, 0:NT], in_=Wf[:, 0:NT], func=AF.Copy), reads=["Wf"], writes=[("Wb", par)])
            nlev = 1 if samp else 5

            def yz_mm(lv):
                a_ = lv % 2
                last = lv == nlev - 1
                for c in range(nch):
                    for h in range(2):
                        hs = slice(64 * h, 64 * h + 64)
                        cs = slice(c * 64, (c + 1) * 64)
                        if not last:
                            S.op("pe", lambda e, hs=hs, cs=cs, h=h, a_=a_: e.matmul(pY[hs, cs], lhsT=Zb[a_][hs, cs], rhs=Yb[a_][hs, cs], start=True, stop=True, tile_position=(64 * h, 64 * h)),
                                 reads=[("Zb", a_), ("Yb", a_)], writes=["pY"])
                        S.op("pe", lambda e, hs=hs, cs=cs, h=h, a_=a_: e.matmul(pZ[hs, cs], lhsT=Yb[a_][hs, cs], rhs=Zb[a_][hs, cs], start=True, stop=True, tile_position=(64 * h, 64 * h)),
                             reads=[("Zb", a_), ("Yb", a_)], writes=["pZ"])

            def yz_ev(lv):
                b_ = (lv + 1) % 2
                if lv != nlev - 1:
                    S.op("act", lambda e, b_=b_: e.activation(out=Yb[b_][:, 0:NT], in_=pY[:, 0:NT], func=AF.Copy), reads=["pY"], writes=[("Yb", b_)])
                S.op("dve", lambda e, b_=b_: e.tensor_copy(out=Zb[b_][:, 0:NT], in_=pZ[:, 0:NT]), reads=["pZ"], writes=[("Zb", b_)])

            def w_mm(lv):
                b_ = (lv + 1) % 2
                for c in range(nch):
                    for h in range(2):
                        hs = slice(64 * h, 64 * h + 64)
                        cs = slice(c * 64, (c + 1) * 64)
                        S.op("pe", lambda e, hs=hs, cs=cs, h=h, b_=b_: e.matmul(pm[hs, cs], lhsT=Zb[b_][hs, cs], rhs=Wb[hs, cs], start=True, stop=True, tile_position=(64 * h, 64 * h)),
                             reads=[("Zb", b_), ("Wb", par)], writes=["pm"])

            def w_ev(lv):
                S.op("dve", lambda e: e.tensor_tensor(out=Wf[:, 0:NT], in0=pm[:, 0:NT], in1=Wf[:, 0:NT], op=ALU.add), reads=["pm", "Wf"], writes=["Wf"])
                S.op("act", lambda e: e.activation(out=Wb[:, 0:NT], in_=Wf[:, 0:NT], func=AF.Copy), reads=["Wf"], writes=[("Wb", par)])

            for lv in range(nlev):
                yz_mm(lv)
                if lv >= 1:
                    w_mm(lv - 1)
                yz_ev(lv)
                if lv >= 1:
                    w_ev(lv - 1)
            w_mm(nlev - 1)
            w_ev(nlev - 1)

        def stageDE(j):
            par = j % 2
            gz, bon, Pt, AR, BK, Vb = B.gz[par], B.bon[par], B.Pt[par], B.AR[par], B.BK[par], B.Vb[par]
            BT, KT, VT, XA, XB, Wb = B.BT[par], B.KT[par], B.VT[par], B.XA[par], B.XB[par], B.Wb[par]
            XAv = XA[:, 0:nch * 128].rearrange("p (c n) -> p c n", n=128)
            XBv = XB[:, 0:nch * 128].rearrange("p (c n) -> p c n", n=128)
            ARv = AR[:, 0:nch * 128].rearrange("p (c two t) -> p c two t", two=2, t=64)
            BKv = BK[:, 0:nch * 128].rearrange("p (c two t) -> p c two t", two=2, t=64)

            def c3(ap):
                return ap.rearrange("p (c t) -> p c t", t=64)

            mark("l0_chain%d" % j)
            G_ = ptrf[:, 0:64]
            U_ = ptrf[:, 64:128]
            Y_ = pch[:, 0:64]
            if not samp:
                D_ = pM[:, 0:64]
                for c in range(nch):
                    cs = slice(c * 64, (c + 1) * 64)
                    p0, p1 = c % 2, (c + 1) % 2
                    for h in range(2):
                        hs = slice(64 * h, 64 * h + 64)
                        tp = (64 * h, 64 * h)
                        S.op("pe", lambda e, hs=hs, tp=tp, c=c, cs=cs: e.matmul(G_[hs, :], lhsT=XBv[hs, c, 0:64], rhs=VT[hs, cs], start=True, stop=False, tile_position=tp),
                             reads=[("XB", par), ("VT", par)], writes=["ptr"])
                    for h in range(2):
                        hs = slice(64 * h, 64 * h + 64)
                        tp = (64 * h, 64 * h)
                        S.op("pe", lambda e, hs=hs, tp=tp, c=c, p0=p0: e.matmul(G_[hs, :], lhsT=ARv[hs, c, 0, :], rhs=SPb[hs, p0, j, :], start=False, stop=True, tile_position=tp),
                             reads=[("AR", par), ("SPb", j, p0)], writes=["ptr"])
                    S.op("act", lambda e: e.activation(out=Gb[:, 0:64], in_=G_, func=AF.Copy), reads=["ptr"], writes=["Gb"])
                    for h in range(2):
                        hs = slice(64 * h, 64 * h + 64)
                        tp = (64 * h, 64 * h)
                        S.op("pe", lambda e, hs=hs, tp=tp, cs=cs: e.matmul(U_[hs, :], lhsT=Wb[hs, cs], rhs=Gb[hs, 0:64], start=True, stop=True, tile_position=tp),
                             reads=[("Wb", par), "Gb"], writes=["ptr"])
                    S.op("act", lambda e: e.activation(out=Ub[:, 0:64], in_=U_, func=AF.Copy), reads=["ptr"], writes=["Ub"])
                    for h in range(2):
                        hs = slice(64 * h, 64 * h + 64)
                        tp = (64 * h, 64 * h)
                        S.op("pe", lambda e, hs=hs, tp=tp, p0=p0: e.matmul(D_[hs, :], lhsT=identb[hs, hs], rhs=SPb[hs, p0, j, :], start=True, stop=False, tile_position=tp),
                             reads=["identb", ("SPb", j, p0)], writes=["pM"])
                        S.op("pe", lambda e, hs=hs, tp=tp, cs=cs: e.matmul(D_[hs, :], lhsT=KT[hs, cs], rhs=VT[hs, cs], start=False, stop=False, tile_position=tp),
                             reads=[("KT", par), ("VT", par)], writes=["pM"])
                    for h in range(2):
                        hs = slice(64 * h, 64 * h + 64)
                        tp = (64 * h, 64 * h)
                        S.op("pe", lambda e, hs=hs, tp=tp, cs=cs: e.matmul(D_[hs, :], lhsT=BT[hs, cs], rhs=Ub[hs, 0:64], start=False, stop=True, tile_position=tp),
                             reads=[("BT", par), "Ub"], writes=["pM"])
                    pcol = Pt[:, c * 64 + 63:c * 64 + 64]
                    S.op("act", lambda e, pcol=pcol, p1=p1: e.activation(out=SPb[:, p1, j, :], in_=D_, func=AF.Copy, scale=pcol), reads=["pM", ("Pt", par)], writes=[("SPb", j, p1)])
                    for h in range(2):
                        hs = slice(64 * h, 64 * h + 64)
                        tp = (64 * h, 64 * h)
                        S.op("pe", lambda e, hs=hs, tp=tp, c=c, cs=cs, p0=p0: e.matmul(pch[hs, cs], lhsT=ARv[hs, c, 1, :], rhs=SPb[hs, p0, j, :], start=True, stop=False, tile_position=tp),
                             reads=[("AR", par), ("SPb", j, p0)], writes=["pch"])
                        S.op("pe", lambda e, hs=hs, tp=tp, c=c, cs=cs: e.matmul(pch[hs, cs], lhsT=XAv[hs, c, 64:128], rhs=Ub[hs, 0:64], start=False, stop=False, tile_position=tp),
                             reads=[("XA", par), "Ub"], writes=["pch"])
                        S.op("pe", lambda e, hs=hs, tp=tp, c=c, cs=cs: e.matmul(pch[hs, cs], lhsT=XBv[hs, c, 64:128], rhs=VT[hs, cs], start=False, stop=True, tile_position=tp),
                             reads=[("XB", par), ("VT", par)], writes=["pch"])
                S.op("act", lambda e: e.activation(out=yTM[:, 0:NT], in_=pch[:, 0:NT], func=AF.Copy), reads=["pch"], writes=["yTM"])
            else:
                Sv = Ssf.rearrange("p (b v) -> p b v", v=64)
                Sbv = Ssb[:, 0:1024].rearrange("p (b v) -> p b v", v=64)
                Snv = Snat.rearrange("p (b v) -> p b v", v=64)
                for b in range(16):
                    S.op("sp", lambda e, b=b: e.dma_start(out=Snv[:, b, :], in_=swkv[b, 2 * j:2 * j + 2, :, :].rearrange("h v k -> (h v) k")), writes=["Snat"], stream="sin")
                for half in range(2):
                    pb = pM
                    for b in range(8):
                        bb_ = half * 8 + b
                        for h in range(2):
                            hs = slice(64 * h, 64 * h + 64)
                            S.op("pe", lambda e, hs=hs, h=h, b=b, bb_=bb_, pb=pb: e.matmul(pb[hs, b * 64:(b + 1) * 64], lhsT=Snv[hs, bb_, :], rhs=cst[hs, O_ID + 64 * h:O_ID + 64 * h + 64],
                                                                                           start=True, stop=True, tile_position=(64 * h, 64 * h)),
                                 reads=["Snat", "cst"], writes=["pM"])
                    S.op("dve", lambda e, half=half, pb=pb: e.tensor_copy(out=Ssf[:, half * 512:(half + 1) * 512], in_=pb[:, :]), reads=["pM"], writes=["Ssf"])
                    S.op("act", lambda e, half=half: e.activation(out=Ssb[:, half * 512:(half + 1) * 512], in_=Ssf[:, half * 512:(half + 1) * 512], func=AF.Copy), reads=["Ssf"], writes=["Ssb"])
                seqm = cst[:, O_SEQ:O_SEQ + 1024].rearrange("p (b t) -> p b t", t=64)
                rowm = cst[:, O_ROW:O_ROW + 16]
                ATv = ATm[:, 0:1024].rearrange("p (b t) -> p b t", t=64)
                RTv = RTm[:, 0:1024].rearrange("p (b t) -> p b t", t=64)
                S.op("dve", lambda e: e.tensor_tensor(out=ATv, in0=ARv[:, 0:1, 0, :].to_broadcast([128, 16, 64]), in1=seqm, op=ALU.mult), reads=[("AR", par), "cst"], writes=["ATm"])
                S.op("dve", lambda e: e.tensor_tensor(out=RTv, in0=ARv[:, 0:1, 1, :].to_broadcast([128, 16, 64]), in1=seqm, op=ALU.mult), reads=[("AR", par), "cst"], writes=["RTm"])
                for h in range(2):
                    hs = slice(64 * h, 64 * h + 64)
                    tp = (64 * h, 64 * h)
                    for b in range(16):
                        S.op("pe", lambda e, hs=hs, tp=tp, b=b: e.matmul(G_[hs, :], lhsT=ATv[hs, b, :], rhs=Sbv[hs, b, :], start=(b == 0), stop=False, tile_position=tp),
                             reads=["ATm", "Ssb"], writes=["ptr"])
                    S.op("pe", lambda e, hs=hs, tp=tp: e.matmul(G_[hs, :], lhsT=XBv[hs, 0, 0:64], rhs=VT[hs, 0:64], start=False, stop=True, tile_position=tp),
                         reads=[("XB", par), ("VT", par)], writes=["ptr"])
                S.op("act", lambda e: e.activation(out=Gb[:, 0:64], in_=G_, func=AF.Copy), reads=["ptr"], writes=["Gb"])
                for h in range(2):
                    hs = slice(64 * h, 64 * h + 64)
                    tp = (64 * h, 64 * h)
                    S.op("pe", lambda e, hs=hs, tp=tp: e.matmul(U_[hs, :], lhsT=Wb[hs, 0:64], rhs=Gb[hs, 0:64], start=True, stop=True, tile_position=tp), reads=[("Wb", par), "Gb"], writes=["ptr"])
                S.op("act", lambda e: e.activation(out=Ub[:, 0:64], in_=U_, func=AF.Copy), reads=["ptr"], writes=["Ub"])
                for h in range(2):
                    hs = slice(64 * h, 64 * h + 64)
                    tp = (64 * h, 64 * h)
                    for b in range(16):
                        S.op("pe", lambda e, hs=hs, tp=tp, b=b: e.matmul(Y_[hs, :], lhsT=RTv[hs, b, :], rhs=Sbv[hs, b, :], start=(b == 0), stop=False, tile_position=tp),
                             reads=["RTm", "Ssb"], writes=["pch"])
                    S.op("pe", lambda e, hs=hs, tp=tp: e.matmul(Y_[hs, :], lhsT=XAv[hs, 0, 64:128], rhs=Ub[hs, 0:64], start=False, stop=False, tile_position=tp), reads=[("XA", par), "Ub"], writes=["pch"])
                    S.op("pe", lambda e, hs=hs, tp=tp: e.matmul(Y_[hs, :], lhsT=XBv[hs, 0, 64:128], rhs=VT[hs, 0:64], start=False, stop=True, tile_position=tp), reads=[("XB", par), ("VT", par)], writes=["pch"])
                S.op("act", lambda e: e.activation(out=yTM[:, 0:64], in_=Y_, func=AF.Copy), reads=["pch"], writes=["yTM"])
                Umv = Um[:, 0:1024].rearrange("p (b v) -> p b v", v=64)
                Vmv = Vm[:, 0:1024].rearrange("p (b v) -> p b v", v=64)
                S.op("dve", lambda e: e.tensor_tensor(out=Umv, in0=Ub[:, 0:64].unsqueeze(1).to_broadcast([128, 16, 64]), in1=rowm.unsqueeze(2).to_broadcast([128, 16, 64]), op=ALU.mult),
                     reads=["Ub", "cst"], writes=["Um"])
                S.op("dve", lambda e: e.tensor_tensor(out=Vmv, in0=VT[:, 0:64].unsqueeze(1).to_broadcast([128, 16, 64]), in1=rowm.unsqueeze(2).to_broadcast([128, 16, 64]), op=ALU.mult),
                     reads=[("VT", par), "cst"], writes=["Vm"])
                P3 = Pt[:, 0:64].rearrange("p (b t) -> p b t", t=4)[:, :, 3:4]
                for half in range(2):
                    pb = pM
                    for h in range(2):
                        hs = slice(64 * h, 64 * h + 64)
                        tp = (64 * h, 64 * h)
                        S.op("pe", lambda e, hs=hs, tp=tp, half=half, pb=pb: e.matmul(pb[hs, :], lhsT=BT[hs, 0:64], rhs=Um[hs, half * 512:(half + 1) * 512], start=True, stop=False, tile_position=tp),
                             reads=[("BT", par), "Um"], writes=["pM"])
                        S.op("pe", lambda e, hs=hs, tp=tp, half=half, pb=pb: e.matmul(pb[hs, :], lhsT=KT[hs, 0:64], rhs=Vm[hs, half * 512:(half + 1) * 512], start=False, stop=True, tile_position=tp),
                             reads=[("KT", par), "Vm"], writes=["pM"])
                    hsl = slice(half * 512, (half + 1) * 512)
                    S.op("dve", lambda e, pb=pb, hsl=hsl: e.tensor_tensor(out=tmpS[:, hsl], in0=pb[:, :], in1=Ssf[:, hsl], op=ALU.add), reads=["pM", "Ssf"], writes=["tmpS"])
                    S.op("dve", lambda e, half=half, hsl=hsl: e.tensor_tensor(out=Ssf[:, hsl].rearrange("p (b v) -> p b v", v=64), in0=tmpS[:, hsl].rearrange("p (b v) -> p b v", v=64),
                                                                               in1=P3[:, half * 8:(half + 1) * 8, :].to_broadcast([128, 8, 64]), op=ALU.mult), reads=["tmpS", ("Pt", par)], writes=["Ssf"])
                for half in range(2):
                    pb = pM
                    for b in range(8):
                        bb_ = half * 8 + b
                        for h in range(2):
                            hs = slice(64 * h, 64 * h + 64)
                            S.op("pe", lambda e, hs=hs, h=h, b=b, bb_=bb_, pb=pb: e.matmul(pb[hs, b * 64:(b + 1) * 64], lhsT=Sv[hs, bb_, :], rhs=cst[hs, O_ID + 64 * h:O_ID + 64 * h + 64],
                                                                                           start=True, stop=True, tile_position=(64 * h, 64 * h)),
                                 reads=["Ssf", "cst"], writes=["pM"])
                    S.op("act", lambda e, half=half, pb=pb: e.activation(out=Snat[:, half * 512:(half + 1) * 512], in_=pb[:, :], func=AF.Copy), reads=["pM"], writes=["Snat"])
                for b in range(16):
                    S.op("sp", lambda e, b=b: e.dma_start(out=swkv_o[b, 2 * j:2 * j + 2, :, :].rearrange("h v k -> (h v) k"), in_=Snv[:, b, :]), reads=["Snat"], stream="out")

            mark("l0_gn%d" % j)
            yv = yTM[:, 0:NT].rearrange("p (c v) -> p c v", v=64)
            S.op("dve", lambda e: e.tensor_reduce(out=stat[:, 32:32 + nch], in_=yv, axis=AX.X, op=ALU.add), reads=["yTM"], writes=[("stat", 32)])
            S.op("act", lambda e: e.activation(out=tmpS[:, 0:NT], in_=yTM[:, 0:NT], func=AF.Square), reads=["yTM"], writes=["tmpS"])
            S.op("dve", lambda e: e.tensor_reduce(out=stat[:, 40:40 + nch], in_=tmpS[:, 0:NT].rearrange("p (c v) -> p c v", v=64), axis=AX.X, op=ALU.add), reads=["tmpS"], writes=[("stat", 40)])
            S.op("dve", lambda e: e.tensor_scalar(out=stat[:, 32:32 + nch], in0=stat[:, 32:32 + nch], scalar1=1.0 / 64, scalar2=None, op0=ALU.mult), reads=[("stat", 32)], writes=[("stat", 32)])
            S.op("dve", lambda e: e.tensor_tensor(out=stat[:, 48:48 + nch], in0=stat[:, 32:32 + nch], in1=stat[:, 32:32 + nch], op=ALU.mult), reads=[("stat", 32)], writes=[("stat", 48)])
            S.op("dve", lambda e: e.scalar_tensor_tensor(out=stat[:, 40:40 + nch], in0=stat[:, 40:40 + nch], scalar=1.0 / 64, in1=stat[:, 48:48 + nch], op0=ALU.mult, op1=ALU.subtract),
                 reads=[("stat", 40), ("stat", 48)], writes=[("stat", 40)])
            S.op("act", lambda e: e.activation(out=stat[:, 40:40 + nch], in_=stat[:, 40:40 + nch], func=AF.Ln, bias=epsk[:, 1:2]), reads=[("stat", 40), "epsk"], writes=[("stat", 40)])
            S.op("act", lambda e: e.activation(out=stat[:, 40:40 + nch], in_=stat[:, 40:40 + nch], func=AF.Exp, scale=-0.5), reads=[("stat", 40)], writes=[("stat", 40)])
            S.op("dve", lambda e: e.tensor_tensor(out=yv, in0=yv, in1=stat[:, 32:32 + nch].unsqueeze(2).to_broadcast([128, nch, 64]), op=ALU.subtract), reads=["yTM", ("stat", 32)], writes=["yTM"])
            S.op("dve", lambda e: e.tensor_tensor(out=ynb[:, 0:NT].rearrange("p (c v) -> p c v", v=64), in0=yv, in1=stat[:, 40:40 + nch].unsqueeze(2).to_broadcast([128, nch, 64]), op=ALU.mult),
                 reads=["yTM", ("stat", 40)], writes=["ynb"])
            for c in range(nch):
                for h in range(2):
                    hs = slice(64 * h, 64 * h + 64)
                    S.op("pe", lambda e, c=c, hs=hs, h=h: e.matmul(ptrf[hs, c * 64:(c + 1) * 64], lhsT=ynb[hs, c * 64:(c + 1) * 64], rhs=identb[hs, hs], start=True, stop=True, tile_position=(64 * h, 64 * h)),
                         reads=["ynb", "identb"], writes=["ptr"])
            S.op("dve", lambda e: e.scalar_tensor_tensor(out=bon[:, 0:NT], in0=ptrf[:, 0:NT], scalar=pc[:, 32 + j:33 + j], in1=bon[:, 0:NT], op0=ALU.mult, op1=ALU.add),
                 reads=["ptr", "pc", ("bon", par)], writes=[("bon", par)])
            S.op("dve", lambda e: e.scalar_tensor_tensor(out=ofm[:, j, 0:NT], in0=bon[:, 0:NT], scalar=pc[:, 48 + j:49 + j], in1=gz[:, 0:NT], op0=ALU.add, op1=ALU.mult),
                 reads=[("bon", par), "pc", ("gz", par)], writes=["ofm"])
        gslots = {0: load_group(win0[:, 128:128 + 512], 512)}
        for _w in range(_DEBUG.get("warm", 0)):
            S.op("pe", lambda e: e.matmul(pch[:, :], lhsT=hT[:, 0, 0:128], rhs=hT[:, 1, 0:512], start=True, stop=True), reads=["hT"], writes=["pch"])

        def rec(fn, j):
            S.buf = []
            fn(j)
            L = S.buf
            S.buf = None
            return L

        def replay(L):
            for a_ in L:
                S.op(*a_)

        def merge(X, Y):
            if not Y:
                return X
            out = []
            nx, ny = len(X), len(Y)
            iy = 0
            for ix in range(nx):
                out.append(X[ix])
                tgt = (ix + 1) * ny // nx
                while iy < tgt:
                    out.append(Y[iy])
                    iy += 1
            out += Y[iy:]
            return out

        replay(rec(stageABC, 0))
        for j in range(16):
            X = rec(stageDE, j)
            Y = rec(stageABC, j + 1) if j + 1 < 16 else []
            replay(merge(X, Y))
        mark("l0_out")
        out_proj(wout0, nsub, RP, NT)

    def layer1(kind, nsub, RP, NT, nseq, T):
        samp = kind == "S"
        mark("l1_start")
        rmsnorm_T(kind, nsub, RP, 1)
        mark("l1_v")
        def vgrp(g4):
            slot = load_group(win1[:, g4 * 512:(g4 + 1) * 512], 512)
            for s in range(nsub):
                pb, pu = nextbank()
                for k in range(8):
                    S.op("pe", lambda e, k=k, s=s, pb=pb: e.matmul(pb[0:RP, :], lhsT=hT[:, k, s * 128:s * 128 + RP], rhs=wg[slot][:, k, :], start=(k == 0), stop=(k == 7)),
                         reads=["hT", ("wg", slot)], writes=[pu])
                col = s * 4 + g4
                if samp:
                    dst = gvs[0:RP, g4 * 512:(g4 + 1) * 512]
                    S.op("act", lambda e, pb=pb, dst=dst, col=col: e.activation(out=dst, in_=pb[0:RP, :], func=AF.Gelu_apprx_tanh, accum_out=stat[0:RP, col:col + 1]),
                         reads=[pu], writes=["gv", ("stat", "a")])
                    S.op("act", lambda e, dst=dst, col=col: e.activation(out=gsc2[0:RP, :], in_=dst, func=AF.Square, accum_out=stat[0:RP, 16 + col:17 + col]),
                         reads=["gv"], writes=["gsc2", ("stat", "b")])
                else:
                    S.op("act", lambda e, pb=pb, col=col: e.activation(out=gsc[0:RP, :], in_=pb[0:RP, :], func=AF.Gelu_apprx_tanh, accum_out=stat[0:RP, col:col + 1]),
                         reads=[pu], writes=["gsc", ("stat", "a")])
                    S.op("act", lambda e, col=col: e.activation(out=gsc2[0:RP, :], in_=gsc[0:RP, :], func=AF.Square, accum_out=stat[0:RP, 16 + col:17 + col]),
                         reads=["gsc"], writes=["gsc2", ("stat", "b")])
                    S.op("dve", lambda e, s=s, g4=g4: e.tensor_copy(out=gvb[0:RP, s * 2048 + g4 * 512:s * 2048 + (g4 + 1) * 512], in_=gsc[0:RP, :]), reads=["gsc"], writes=["gv"])
        for g4 in range(4):
            vgrp(g4)
        st4 = stat[0:RP, 0:4 * nsub].rearrange("p (s g) -> p s g", g=4)
        sq4 = stat[0:RP, 16:16 + 4 * nsub].rearrange("p (s g) -> p s g", g=4)
        S.op("dve", lambda e: e.tensor_reduce(out=stat[0:RP, 32:32 + nsub], in_=st4, axis=AX.X, op=ALU.add), reads=[("stat", "a")], writes=[("stat", 32)])
        S.op("dve", lambda e: e.tensor_reduce(out=stat[0:RP, 40:40 + nsub], in_=sq4, axis=AX.X, op=ALU.add), reads=[("stat", "b")], writes=[("stat", 40)])
        S.op("dve", lambda e: e.tensor_scalar(out=stat[0:RP, 32:32 + nsub], in0=stat[0:RP, 32:32 + nsub], scalar1=1.0 / E, scalar2=None, op0=ALU.mult), reads=[("stat", 32)], writes=[("stat", 32)])
        S.op("dve", lambda e: e.tensor_tensor(out=stat[0:RP, 48:48 + nsub], in0=stat[0:RP, 32:32 + nsub], in1=stat[0:RP, 32:32 + nsub], op=ALU.mult), reads=[("stat", 32)], writes=[("stat", 48)])
        S.op("dve", lambda e: e.scalar_tensor_tensor(out=stat[0:RP, 40:40 + nsub], in0=stat[0:RP, 40:40 + nsub], scalar=1.0 / E, in1=stat[0:RP, 48:48 + nsub], op0=ALU.mult, op1=ALU.subtract),
             reads=[("stat", 40), ("stat", 48)], writes=[("stat", 40)])
        S.op("dve", lambda e: e.tensor_scalar(out=stat[0:RP, 40:40 + nsub], in0=stat[0:RP, 40:40 + nsub], scalar1=1e-5, scalar2=None, op0=ALU.add), reads=[("stat", 40)], writes=[("stat", 40)])
        S.op("act", lambda e: e.activation(out=stat[0:RP, 40:40 + nsub], in_=stat[0:RP, 40:40 + nsub], func=AF.Sqrt), reads=[("stat", 40)], writes=[("stat", 40)])
        S.op("dve", lambda e: e.reciprocal(out=stat[0:RP, 40:40 + nsub], in_=stat[0:RP, 40:40 + nsub]), reads=[("stat", 40)], writes=[("stat", 40)])
        S.op("dve", lambda e: e.scalar_tensor_tensor(out=stat[0:RP, 48:48 + nsub], in0=stat[0:RP, 32:32 + nsub], scalar=-1.0, in1=stat[0:RP, 40:40 + nsub], op0=ALU.mult, op1=ALU.mult),
             reads=[("stat", 32), ("stat", 40)], writes=[("stat", 48)])
        for s in range(nsub):
            if samp:
                vsrc = gvs[0:RP, :]
                vmid = gvs[0:RP, :]
                vdst = gvs[0:RP, :]
            else:
                vsrc = gvb[0:RP, s * 2048:(s + 1) * 2048]
                vmid = t32[0:RP, :]
                vdst = vnb[0:RP, s * 2048:(s + 1) * 2048]
            S.op("dve", lambda e, s=s, vsrc=vsrc, vmid=vmid: e.tensor_scalar(out=vmid, in0=vsrc, scalar1=stat[0:RP, 40 + s:41 + s], scalar2=stat[0:RP, 48 + s:49 + s], op0=ALU.mult, op1=ALU.add),
                 reads=["gv", ("stat", 40), ("stat", 48)], writes=["t32"])
            S.op("dve", lambda e, vmid=vmid: e.tensor_tensor(out=vmid, in0=vmid, in1=vgbc[0:RP, 0, :], op=ALU.mult), reads=["t32", "vgbc"], writes=["t32"])
            S.op("dve", lambda e, vmid=vmid, vdst=vdst: e.tensor_tensor(out=vdst, in0=vmid, in1=vgbc[0:RP, 1, :], op=ALU.add), reads=["t32", "vgbc"], writes=["vn"])
        if samp:
            S.op("sp", lambda e: e.dma_start(out=sv, in_=gvs[0:64, :]), reads=["vn"], stream="out")
            S.op("act", lambda e: e.activation(out=gvb[0:64, 0:2048], in_=gvs[0:64, :], func=AF.Copy), reads=["vn"], writes=["gvb2"])
        mark("l1_uz")
        def uzgrp(jj):
            slot = load_group(win1[:, 2048 + jj * 512:2048 + (jj + 1) * 512], 512)
            for q in range(2):
                j = 2 * jj + q
                g = j // 2
                pq = j % 2
                gu, zt1, gz1, o1t = gu2[pq], zt12[pq], gz12[pq], o1t2[pq]
                banks = [nextbank(), nextbank()]
                for i in range(2):
                    pb, pu = banks[i]
                    for k in range(8):
                        S.op("pe", lambda e, i=i, k=k, q=q, pb=pb: e.matmul(pb[:, 0:NT], lhsT=wg[slot][:, k, (q * 2 + i) * 128:(q * 2 + i + 1) * 128], rhs=hT[:, k, 0:NT], start=(k == 0), stop=(k == 7)),
                             reads=["hT", ("wg", slot)], writes=[pu])
                (pbu, puu), (pbz, puz) = banks
                S.op("act", lambda e, pbu=pbu, gu=gu: e.activation(out=gu[:, 0:NT], in_=pbu[:, 0:NT], func=AF.Gelu_apprx_tanh), reads=[puu], writes=[("gu", pq)])
                S.op("act", lambda e, pbz=pbz, zt1=zt1: e.activation(out=zt1[:, 0:NT], in_=pbz[:, 0:NT], func=AF.Tanh, scale=0.5), reads=[puz], writes=[("zt1", pq)])
                S.op("dve", lambda e, pbz=pbz, zt1=zt1, gz1=gz1: e.scalar_tensor_tensor(out=gz1[:, 0:NT], in0=zt1[:, 0:NT], scalar=1.0, in1=pbz[:, 0:NT], op0=ALU.add, op1=ALU.mult),
                     reads=[("zt1", pq), puz], writes=[("gz1", pq)])
                pbm, pum = nextbank()
                if samp:
                    S.op("pe", lambda e, j=j, g=g, pbm=pbm: e.matmul(pbm[:, 0:64], lhsT=gvb[0:64, j * 128:(j + 1) * 128], rhs=bdtb[0:64, g, :], start=True, stop=False), reads=["gvb2", "bdtb"], writes=[pum])
                    S.op("pe", lambda e, g=g, pbm=pbm: e.matmul(pbm[:, 0:64], lhsT=onesr[0:1, :], rhs=bsS[0:1, g, :], start=False, stop=True), reads=["onesr", "bsS"], writes=[pum])
                else:
                    for s in range(nsub):
                        S.op("pe", lambda e, j=j, g=g, s=s, pbm=pbm: e.matmul(pbm[:, s * 128:(s + 1) * 128], lhsT=vnb[:, s * 2048 + j * 128:s * 2048 + (j + 1) * 128], rhs=wmT[:, g, :], start=True, stop=False),
                             reads=["vn", "wmT"], writes=[pum])
                        S.op("pe", lambda e, g=g, s=s, pbm=pbm: e.matmul(pbm[:, s * 128:(s + 1) * 128], lhsT=onesr[0:1, :], rhs=bsP[0:1, g * 128:(g + 1) * 128], start=False, stop=True),
                             reads=["onesr", "bsP"], writes=[pum])
                S.op("dve", lambda e, pbm=pbm, gu=gu, o1t=o1t: e.tensor_tensor(out=o1t[:, 0:NT], in0=pbm[:, 0:NT], in1=gu[:, 0:NT], op=ALU.mult), reads=[pum, ("gu", pq)], writes=[("o1t", pq)])
                S.op("dve", lambda e, j=j, o1t=o1t, gz1=gz1: e.tensor_tensor(out=ofm[:, j, 0:NT], in0=o1t[:, 0:NT], in1=gz1[:, 0:NT], op=ALU.mult), reads=[("o1t", pq), ("gz1", pq)], writes=["ofm"])
        for jj in range(8):
            uzgrp(jj)
        mark("l1_out")
        out_proj(wout1, nsub, RP, NT)
        mark("l1_fin")

    def final_norm(kind, nsub, RP, tok0):
        dst = ys if kind == "S" else yp
        for s in range(nsub):
            S.op("act", lambda e, s=s: e.activation(out=hb[0:RP, :], in_=xres[0:RP, s, :], func=AF.Square, accum_out=stat[0:RP, s:s + 1]),
                 reads=[("xres", s)], writes=["hb", ("stat", s)])
            S.op("dve", lambda e, s=s: e.tensor_scalar(out=stat[0:RP, 8 + s:9 + s], in0=stat[0:RP, s:s + 1], scalar1=1.0 / D, scalar2=1e-6, op0=ALU.mult, op1=ALU.add),
                 reads=[("stat", s)], writes=[("stat", 8 + s)])
            S.op("act", lambda e, s=s: e.activation(out=stat[0:RP, 16 + s:17 + s], in_=stat[0:RP, 8 + s:9 + s], func=AF.Sqrt), reads=[("stat", 8 + s)], writes=[("stat", 16 + s)])
            S.op("dve", lambda e, s=s: e.reciprocal(out=stat[0:RP, 24 + s:25 + s], in_=stat[0:RP, 16 + s:17 + s]), reads=[("stat", 16 + s)], writes=[("stat", 24 + s)])
            S.op("dve", lambda e, s=s: e.scalar_tensor_tensor(out=xres[0:RP, s, :], in0=xres[0:RP, s, :], scalar=stat[0:RP, 24 + s:25 + s], in1=gbc[0:RP, 2, :], op0=ALU.mult, op1=ALU.mult),
                 reads=[("xres", s), ("stat", 24 + s), "gbc"], writes=[("xres", s)])
            S.op("sp", lambda e, s=s: e.dma_start(out=dst[tok0 + s * 128:tok0 + s * 128 + RP, :], in_=xres[0:RP, s, :]), reads=[("xres", s)], stream="out")

    for tname in tiles:
        if tname == "S":
            kind, nsub, RP, NT, nseq, T, tok0 = "S", 1, 64, 64, 16, 4, 0
            S.fence()
            S.op("sp", lambda e: e.dma_start(out=sTM[0:16, :], in_=sshift), writes=["sTM"], stream="sin2")
            for cidx in range(50):
                if cidx < 48:
                    col0, w = (cidx // 16) * 2048 + (cidx % 16) * 128, 128
                else:
                    col0, w = 6144 + (cidx - 48) * 64, 64
                S.op("pe", lambda e, col0=col0, w=w: e.transpose(out=pm[0:w, 0:16], in_=sTM[0:16, col0:col0 + w], identity=cst[0:16, O_ID:O_ID + 16]), reads=["sTM", "cst"], writes=["pm"])
                S.op("act", lambda e, cidx=cidx, w=w: e.activation(out=sinT[0:w, cidx, :], in_=pm[0:w, 0:16], func=AF.Copy), reads=["pm"], writes=["sinT"])
            S.fence()
            src = xs
        else:
            kind, nsub, RP, NT, nseq, T = "P", 4, 128, 512, 1, 512
            tok0 = int(tname[1]) * 512
            src = xp
        for s in range(nsub):
            S.op("sp", lambda e, s=s, src=src, tok0=tok0, RP=RP: e.dma_start(out=xres[0:RP, s, :], in_=src[tok0 + s * 128:tok0 + s * 128 + RP, :]), writes=[("xres", s)], stream="xin")
        layer0(kind, nsub, RP, NT, nseq, T)
        S.fence()
        if _DEBUG.get("l0_only"):
            dst_ = ys if kind == "S" else yp
            for s in range(nsub):
                S.op("sp", lambda e, s=s, dst_=dst_, tok0=tok0, RP=RP: e.dma_start(out=dst_[tok0 + s * 128:tok0 + s * 128 + RP, :], in_=xres[0:RP, s, :]), reads=[("xres", s)], stream="out")
            continue
        layer1(kind, nsub, RP, NT, nseq, T)
        final_norm(kind, nsub, RP, tok0)
        S.fence()

    S.op("sp", lambda e: e.dma_start(out=pshift, in_=carryP[:, :, 0]), reads=["carryP"], stream="out")
    S.op("dve", lambda e: e.tensor_copy(out=SPf.rearrange("p (j v) -> p j v", v=64), in_=SPb[:, 0, :, :]),
         reads=[("SPb", j, 0) for j in range(16)], writes=["SPf"])
    for half in range(2):
        for jj in range(8):
            j = half * 8 + jj
            for h in range(2):
                hs = slice(64 * h, 64 * h + 64)
                S.op("pe", lambda e, hs=hs, h=h, j=j, jj=jj, half=half: e.matmul(pj[half][hs, jj * 64:(jj + 1) * 64], lhsT=SPf[hs, j * 64:(j + 1) * 64], rhs=cst[hs, O_ID + 64 * h:O_ID + 64 * h + 64], start=True, stop=True,
                                                                                 tile_position=(64 * h, 64 * h)), reads=["SPf", "cst"], writes=[("pj", half)])
        S.op("act", lambda e, half=half: e.activation(out=SnatE[:, half * 512:(half + 1) * 512], in_=pj[half][:, :], func=AF.Copy), reads=[("pj", half)], writes=["SnatE"])
    S.op("sp", lambda e: e.dma_start(out=pwkv.rearrange("(j h) v k -> (h v) j k", h=2), in_=SnatE.rearrange("p (j k) -> p j k", k=64)), reads=["SnatE"], stream="out")
    if "S" in tiles:
        for cidx in range(50):
            if cidx < 48:
                col0, w = (cidx // 16) * 2048 + (cidx % 16) * 128, 128
            else:
                col0, w = 6144 + (cidx - 48) * 64, 64
            S.op("pe", lambda e, cidx=cidx, w=w: e.transpose(out=pm[0:16, 0:w], in_=soutT[0:w, cidx, :], identity=cst[0:w, O_ID:O_ID + w]), reads=["soutT", "cst"], writes=["pm"])
            S.op("act", lambda e, col0=col0, w=w: e.activation(out=sTM[0:16, col0:col0 + w], in_=pm[0:16, 0:w], func=AF.Copy), reads=["pm"], writes=["sTM"])
        S.op("sp", lambda e: e.dma_start(out=sshift_o, in_=sTM[0:16, :]), reads=["sTM"], stream="out")
    _DEBUG["nops"] = len(S.all_ops)
    _DEBUG["marks"] = marks
    S.emit(final_wait_streams=["out"] if "out" in S.streams else [])
    _DEBUG["sigcnts"] = S.sigcnts
    _DEBUG["streams"] = {k: v[0] for k, v in S.streams.items()}
    return nc


def make_consts():
    c = np.zeros((128, NCST), np.float32)
    c[:, O_ID:O_ID + 128] = np.eye(128)
    for h in range(2):
        c[64 * h:64 * h + 64, O_BO + 64 * h:O_BO + 64 * h + 64] = 1.0
    i = np.arange(64)[:, None]
    t = np.arange(64)[None, :]
    su = (i < t).astype(np.float32)
    ui = (i <= t).astype(np.float32)
    sl = (t < i).astype(np.float32)
    same = ((i // 4) == (t // 4)).astype(np.float32)
    for h in range(2):
        r = slice(64 * h, 64 * h + 64)
        c[r, O_M2P:O_M2P + 64] = su
        c[r, O_M2P + 64:O_M2P + 128] = ui
        c[r, O_MLP:O_MLP + 64] = sl
        c[r, O_M2S:O_M2S + 64] = su * same
        c[r, O_M2S + 64:O_M2S + 128] = ui * same
        c[r, O_MLS:O_MLS + 64] = sl * same
    rp = np.ones(512, np.float32)
    rp[::64] = 0
    c[:, O_RSTP:O_RSTP + 512] = rp
    rs = np.ones(64, np.float32)
    rs[::4] = 0
    c[:, O_RSTS:O_RSTS + 64] = rs
    seq = np.zeros((16, 64), np.float32)
    for b in range(16):
        seq[b, 4 * b:4 * b + 4] = 1
    c[:, O_SEQ:O_SEQ + 1024] = seq.reshape(1, 1024)
    p = np.arange(128)
    row = np.zeros((128, 16), np.float32)
    row[p, (p % 64) // 4] = 1
    c[:, O_ROW:O_ROW + 16] = row
    s_ = np.arange(128)[:, None]
    t_ = np.arange(128)[None, :]
    c[:, O_TRIU:O_TRIU + 128] = (s_ <= t_).astype(np.float32)
    return c


_NC_CACHE = {}
_DEBUG = {}


def kernel(x_prompt, x_sample, state_shift, state_wkv, norm_g, norm_f, rw_in, rw_mu, rw_w0, rw_w2, rw_a0, rw_a2,
           rw_kk, rw_ka, rw_rk, rw_lnx_g, rw_lnx_b, rw_out, gm_in, gm_vg, gm_vb, gm_ws, gm_bs, gm_out):
    f = lambda a: np.ascontiguousarray(np.asarray(a, dtype=np.float32))
    x_prompt, x_sample, state_shift, state_wkv = f(x_prompt), f(x_sample), f(state_shift), f(state_wkv)
    perm0 = list(range(6144, 6272))
    for j in range(16):
        for base in (0, 2048, 4096, 6272):
            perm0 += list(range(base + j * 128, base + (j + 1) * 128))
    win0 = f(f(rw_in)[0][:, perm0])
    perm1 = list(range(2048, 4096))
    for j in range(16):
        for base in (0, 4096):
            perm1 += list(range(base + j * 128, base + (j + 1) * 128))
    win1 = f(f(gm_in)[0][:, perm1])
    fm = lambda v: f(v).reshape(16, 128).T
    mu = f(rw_mu)[0]
    ppn = np.zeros((128, 130), np.float32)
    ppn[:, 0:16] = fm(mu[0:2048])
    ppn[:, 16:32] = fm(mu[2048:4096])
    ppn[:, 32:48] = fm(mu[4096:6144])
    ppn[:, 48:64] = fm(f(rw_kk)[0])
    ppn[:, 64:80] = fm(f(rw_ka)[0])
    ppn[:, 80:96] = fm(f(rw_lnx_g)[0])
    ppn[:, 96:112] = fm(f(rw_lnx_b)[0])
    ppn[:, 112:128] = fm(f(rw_rk)[0].reshape(-1))
    ppn[0:64, 128] = mu[6144:6208]
    ppn[0:64, 129] = mu[6208:6272]
    w2aug = f(np.concatenate([f(rw_w2)[0], f(rw_w0)], axis=0))
    a2aug = f(np.concatenate([f(rw_a2)[0], f(rw_a0)], axis=0))
    gnn = f(np.concatenate([f(norm_g), f(norm_f)[None, :]], axis=0))
    vgbn = f(np.concatenate([f(gm_vg), f(gm_vb)], axis=0))
    shared = {
        "gn": gnn, "win0": win0, "pp": ppn, "w2aug": w2aug, "a2aug": a2aug, "wout0": f(f(rw_out)[0]),
        "win1": win1, "vgb": vgbn, "ws": f(f(gm_ws)[0]), "bs": f(f(gm_bs)[0].reshape(1, 1024)),
        "wout1": f(f(gm_out)[0]), "cst": make_consts(),
    }
    in_maps = []
    for c in range(NCORES):
        m = dict(shared)
        m["xp"] = f(x_prompt[c])
        m["xs"] = f(x_sample[16 * c:16 * c + 16].reshape(64, D))
        m["sshift"] = f(state_shift[0, 16 * c:16 * c + 16])
        m["swkv"] = f(state_wkv[0, 16 * c:16 * c + 16])
        in_maps.append(m)
    if _DEBUG.get("maps_only"):
        return in_maps
    if "nc" not in _NC_CACHE:
        _NC_CACHE["nc"] = build_nc()
    nc = _NC_CACHE["nc"]
    res = run_bass_kernel_spmd(nc, in_maps, core_ids=list(range(NCORES)))
    R = res.results
    y_prompt = np.stack([R[c]["yp"] for c in range(NCORES)]).astype(np.float32)
    y_sample = np.concatenate([R[c]["ys"].reshape(16, 4, D) for c in range(NCORES)]).astype(np.float32)
    pshift = np.zeros((1, NCORES, SHIFT), np.float32)
    for c in range(NCORES):
        ps = R[c]["pshift"]
        for i in range(3):
            pshift[0, c, i * 2048:(i + 1) * 2048] = ps[:, i * 16:(i + 1) * 16].T.reshape(-1)
        pshift[0, c, 6144:6208] = ps[0:64, 48]
        pshift[0, c, 6208:6272] = ps[0:64, 49]
    prompt_wkv = np.stack([R[c]["pwkv"] for c in range(NCORES)])[None].astype(np.float32)
    sample_shift = np.concatenate([R[c]["sshift_o"] for c in range(NCORES)])[None].astype(np.float32)
    sample_wkv = np.concatenate([R[c]["swkv_o"] for c in range(NCORES)])[None].astype(np.float32)
    sample_v = np.concatenate([R[c]["sv"].reshape(16, 4, E) for c in range(NCORES)])[None].astype(np.float32)
    return (y_prompt, y_sample, pshift, prompt_wkv, sample_shift, sample_wkv, sample_v)
```

```python
import contextlib
import numpy as np
import concourse.bass as bass
import concourse.mybir as mybir
from concourse.bass_utils import run_bass_kernel_spmd

F32 = mybir.dt.float32
BF16 = mybir.dt.bfloat16
AF = mybir.ActivationFunctionType
ALU = mybir.AluOpType
AX = mybir.AxisListType

NCORES = 8
D = 1024
E = 2048
SHIFT = 6272
C_LW = -0.5 * float(np.exp(-0.5))

O_ID = 0
O_BO = 128
O_M2P = 256
O_MLP = 384
O_M2S = 448
O_MLS = 576
O_RSTP = 640
O_RSTS = 1152
O_SEQ = 1216
O_ROW = 2240
O_TRIU = 2256
NCST = 2384


class Unit:
    __slots__ = ("w", "rs")

    def __init__(self):
        self.w = None
        self.rs = []


class Op:
    __slots__ = ("eng", "fn", "deps", "sig", "cnt", "stream", "scnt", "idx", "extra", "sneed")


class Sched:
    ENGS = ("pe", "act", "dve", "pool", "sp")

    def __init__(self, nc):
        self.nc = nc
        self.ops = {e: [] for e in self.ENGS}
        self.units = {}
        self.streams = {}
        self.all_ops = []

    def U(self, key):
        u = self.units.get(key)
        if u is None:
            u = self.units[key] = Unit()
        return u

    maxops = None
    buf = None

    def op(self, eng, fn, reads=(), writes=(), stream=None):
        if self.buf is not None:
            self.buf.append((eng, fn, tuple(reads), tuple(writes), stream))
            return None
        if self.maxops is not None and len(self.all_ops) >= self.maxops:
            return None
        o = Op()
        o.eng = eng
        o.fn = fn
        o.sig = False
        o.cnt = None
        o.stream = stream
        o.scnt = None
        o.extra = None
        deps = set()
        rkeys = set(reads)
        for k in reads:
            u = self.U(k)
            if u.w is not None:
                deps.add(u.w)
            u.rs.append(o)
        for k in writes:
            u = self.U(k)
            if u.w is not None and (k in rkeys or u.w.stream is not None or stream is not None or u.w.eng != eng):
                deps.add(u.w)
            for r in u.rs:
                if r is not o and (r.stream is not None or stream is not None or r.eng != eng):
                    deps.add(r)
            u.w = o
            u.rs = []
        if stream is not None:
            s = self.streams.setdefault(stream, [0])
            s[0] += 1
            o.scnt = s[0]
        dl = []
        sneed = {}
        for d in deps:
            if d.stream is None and d.eng == eng and eng == "pe":
                continue
            if d.stream is not None:
                sneed[d.stream] = self.streams[d.stream][0] - (1 if d.stream == stream else 0)
            dl.append(d)
        o.deps = dl
        o.sneed = sneed
        o.idx = len(self.all_ops)
        self.all_ops.append(o)
        self.ops[eng].append(o)
        return o

    def fence(self):
        if self.maxops is not None and len(self.all_ops) >= self.maxops:
            return
        last = {}
        for e in self.ENGS:
            for o in reversed(self.ops[e]):
                if o.stream is None and o.fn is not None:
                    last[e] = o
                    break
        snap = {s: v[0] for s, v in self.streams.items()}
        for e in self.ENGS:
            o = Op()
            o.eng = e
            o.fn = None
            o.sig = False
            o.cnt = None
            o.stream = None
            o.scnt = None
            o.deps = [d for k, d in last.items() if not (k == e and e == "pe")]
            o.extra = snap
            o.sneed = None
            o.idx = len(self.all_ops)
            self.all_ops.append(o)
            self.ops[e].append(o)

    def emit(self, final_wait_streams=()):
        nc = self.nc
        for o in self.all_ops:
            for d in o.deps:
                if d.stream is None:
                    d.sig = True
        cnts = {e: 0 for e in self.ENGS}
        for o in self.all_ops:
            if o.stream is None and o.sig:
                cnts[o.eng] += 1
                o.cnt = cnts[o.eng]
        self.sigcnts = cnts
        with contextlib.ExitStack() as st:
            esem = {e: st.enter_context(nc.semaphore("s_" + e)) for e in self.ENGS}
            ssem = {s: st.enter_context(nc.semaphore("d_" + s)) for s in self.streams}
            block = st.enter_context(nc.Block())
            sched = self

            def run(engname, e):
                seen = {}
                for o in sched.ops[engname]:
                    need = {}
                    for d in o.deps:
                        if d.stream is not None:
                            key = ("s", d.stream)
                            v = max(d.scnt, (o.sneed or {}).get(d.stream, 0)) * 16
                        else:
                            key = ("e", d.eng)
                            v = d.cnt
                        if v > need.get(key, 0):
                            need[key] = v
                    if o.extra:
                        for s, c in o.extra.items():
                            if c * 16 > need.get(("s", s), 0):
                                need[("s", s)] = c * 16
                    for key, v in need.items():
                        if seen.get(key, 0) >= v:
                            continue
                        seen[key] = v
                        sem = ssem[key[1]] if key[0] == "s" else esem[key[1]]
                        e.wait_ge(sem, v)
                    if o.fn is None:
                        continue
                    ins = o.fn(e)
                    if o.stream is not None:
                        ins.then_inc(ssem[o.stream], 16)
                    elif o.sig:
                        ins.then_inc(esem[engname], 1)
                if engname == "sp":
                    for s in final_wait_streams:
                        e.wait_ge(ssem[s], sched.streams[s][0] * 16)

            @block.tensor
            def _(e):
                run("pe", e)

            @block.scalar
            def _(e):
                run("act", e)

            @block.vector
            def _(e):
                run("dve", e)

            @block.gpsimd
            def _(e):
                run("pool", e)

            @block.sync
            def _(e):
                run("sp", e)


def build_nc(tiles=("P0", "P1", "P2", "P3", "S")):
    nc = bass.Bass("TRN2", target_bir_lowering=False)
    S = Sched(nc)
    S.maxops = _DEBUG.get("maxops")

    def din(name, shape):
        return nc.dram_tensor(name, list(shape), F32, kind="ExternalInput").ap()

    def dout(name, shape):
        return nc.dram_tensor(name, list(shape), F32, kind="ExternalOutput").ap()

    xp = din("xp", [2048, D])
    xs = din("xs", [64, D])
    sshift = din("sshift", [16, SHIFT])
    swkv = din("swkv", [16, 32, 64, 64])
    gn = din("gn", [3, D])
    win0 = din("win0", [D, 8320])
    pp = din("pp", [128, 130])
    w2aug = din("w2aug", [65, E])
    a2aug = din("a2aug", [65, E])
    wout0 = din("wout0", [E, D])
    win1 = din("win1", [D, 6144])
    vgb = din("vgb", [2, E])
    wsd = din("ws", [8, 128, 128])
    bsd = din("bs", [1, 1024])
    wout1 = din("wout1", [E, D])
    cstd = din("cst", [128, NCST])

    yp = dout("yp", [2048, D])
    ys = dout("ys", [64, D])
    pshift = dout("pshift", [128, 50])
    pwkv = dout("pwkv", [32, 64, 64])
    sshift_o = dout("sshift_o", [16, SHIFT])
    swkv_o = dout("swkv_o", [16, 32, 64, 64])
    sv = dout("sv", [64, E])

    def sb(name, shape, dt=F32):
        return nc.alloc_sbuf_tensor("sb_" + name, list(shape), dt)

    def psum(name, shape, dt=F32):
        return nc.alloc_psum_tensor("ps_" + name, list(shape), dt)

    cst = sb("cst", [128, NCST])
    identb = sb("identb", [128, 128], BF16)
    bones = sb("bones", [128, 128], BF16)
    ppt = sb("ppt", [128, 130])
    pc = sb("pc", [128, 80])
    bdrk = sb("bdrk", [128, 16, 128], BF16)
    w2b = sb("w2b", [65, E], BF16)
    a2b = sb("a2b", [65, E], BF16)
    gbc = sb("gbc", [128, 3, D])
    vgbc = sb("vgbc", [128, 2, E])
    wmT = sb("wmT", [128, 8, 128], BF16)
    bdtb = sb("bdtb", [64, 8, 64], BF16)
    bsP = sb("bsP", [1, 1024], BF16)
    bsS = sb("bsS", [1, 8, 64], BF16)
    onesr = sb("onesr", [1, 128], BF16)
    carryP = sb("carryP", [128, 50, 1])
    SPb = sb("SPb", [128, 2, 16, 64], BF16)
    xres = sb("xres", [128, 4, D])
    hb = sb("hb", [128, D], BF16)
    hT = sb("hT", [128, 8, 512], BF16)
    NWG = 2
    wg = [sb("wg%d" % i, [128, 8, 512], BF16) for i in range(NWG)]
    ofm = sb("ofm", [128, 16, 512], BF16)
    stat = sb("stat", [128, 64])
    epsk = sb("epsk", [128, 2])
    twd = sb("twd", [65, 512], BF16)
    adg = sb("adg", [65, 512], BF16)
    ARENA = 20608
    arena = sb("arena", [128, ARENA])

    class Carve:
        def __init__(self, off=0):
            self.off = off

        def f32(self, n, rows=128):
            a = arena[0:rows, self.off:self.off + n]
            self.off += n
            assert self.off <= ARENA, self.off
            return a

        def bf(self, n):
            m = (n + 1) // 2
            a = arena[:, self.off:self.off + m].bitcast(BF16)
            self.off += m
            assert self.off <= ARENA, self.off
            return a

    class NS:
        pass

    sinT = arena[:, ARENA - 1600:ARENA - 800].rearrange("p (c b) -> p c b", b=16)
    soutT = arena[:, ARENA - 800:ARENA].rearrange("p (c b) -> p c b", b=16)

    def carve_l0(samp):
        N = 64 if samp else 512
        c0 = Carve()
        B = NS()
        B.sh = c0.f32(3 * 520)
        B.xm = c0.f32(3 * 512)
        lo = c0.off
        for nm in ("lw", "Lc", "av", "sq", "kkn", "k2", "bbv", "Pinv", "Pm", "yTM", "Wf"):
            setattr(B, nm, c0.f32(N))
        B.gz = [c0.f32(N), c0.f32(N)]
        B.bon = [c0.f32(N), c0.f32(N)]
        B.Pt = [c0.f32(N), c0.f32(N)]
        B.tmpS = c0.f32(1024 if samp else 512)
        if samp:
            B.Ssf = c0.f32(1024)
            B.Snat = c0.f32(1024)
        B.rkb = c0.bf(N)
        B.AR = [c0.bf(2 * N), c0.bf(2 * N)]
        B.BK = [c0.bf(2 * N), c0.bf(2 * N)]
        B.Vb = [c0.bf(N), c0.bf(N)]
        B.BT = [c0.bf(N), c0.bf(N)]
        B.KT = [c0.bf(N), c0.bf(N)]
        B.VT = [c0.bf(N), c0.bf(N)]
        B.XA = [c0.bf(2 * N), c0.bf(2 * N)]
        B.XB = [c0.bf(2 * N), c0.bf(2 * N)]
        B.Yb = [c0.bf(N), c0.bf(N)]
        B.Zb = [c0.bf(N), c0.bf(N)]
        B.Wb = [c0.bf(N), c0.bf(N)]
        B.Gb = c0.bf(64)
        B.Ub = c0.bf(64)
        B.ynb = c0.bf(N)
        if samp:
            B.Ssb = c0.bf(1024)
            B.ATm = c0.bf(1024)
            B.RTm = c0.bf(1024)
            B.Um = c0.bf(1024)
            B.Vm = c0.bf(1024)
        cl = Carve(lo)
        B.shl = cl.f32(1040, rows=64).rearrange("p (i n) -> p i n", n=520)
        B.xml = cl.f32(1024, rows=64).rearrange("p (i n) -> p i n", n=512)
        B.dl = cl.f32(1024, rows=64).rearrange("p (i n) -> p i n", n=512)
        return B

    c1 = Carve()
    gvs = c1.f32(2048)
    gsc = c1.f32(512)
    gsc2 = c1.f32(512)
    gu2 = [c1.f32(512), c1.f32(512)]
    zt12 = [c1.f32(512), c1.f32(512)]
    gz12 = [c1.f32(512), c1.f32(512)]
    o1t2 = [c1.f32(512), c1.f32(512)]
    gvb = c1.bf(4 * 2048)
    t32 = c1.f32(2048)
    vnb = c1.bf(4 * 2048)
    c2 = Carve()
    sTM = c2.f32(SHIFT)
    c3_ = Carve()
    wmTs = c3_.f32(1024).rearrange("p (g t) -> p g t", t=128)
    wnat = c3_.f32(1024).rearrange("p (g t) -> p g t", t=128)
    bdt = c3_.f32(512, rows=64).rearrange("p (g t) -> p g t", t=64)
    bsf = c3_.f32(1024, rows=1)
    c4 = Carve(SHIFT)
    SnatE = c4.f32(1024)
    SPf = c4.f32(1024)

    pj = [psum("pjA", [128, 512]), psum("pjB", [128, 512])]
    pm = psum("pm", [128, 512])
    ptrf = psum("ptr", [128, 512])
    ptr = ptrf[:, :].bitcast(BF16)
    pM = psum("pM", [128, 512])
    pY = psum("pY", [128, 512])
    pZ = psum("pZ", [128, 512])
    pch = psum("pch", [128, 512])

    ALLB = [(pj[0], ("pj", 0)), (pj[1], ("pj", 1)), (pm, "pm"), (pM, "pM"), (pY, "pY"), (pZ, "pZ"), (pch, "pch")]
    brot = [0]

    def nextbank():
        b_ = ALLB[brot[0] % len(ALLB)]
        brot[0] += 1
        return b_

    wcnt = [0]

    def load_group(src_ap, ncols):
        slot = wcnt[0] % NWG
        wcnt[0] += 1
        t = wg[slot]
        S.op("pool", lambda e: e.dma_start(out=t[:, :, 0:ncols], in_=src_ap.rearrange("(k p) c -> p k c", p=128)),
             writes=[("wg", slot)], stream="wg%d" % slot)
        return slot

    def cval(off, n, rows=128):
        return cst[0:rows, off:off + n]

    marks = {}

    def mark(name):
        marks.setdefault(name, len(S.all_ops))

    S.op("sp", lambda e: e.dma_start(out=cst[:, :], in_=cstd), writes=["cst"], stream="c0")
    S.op("sp", lambda e: e.dma_start(out=ppt[:, :], in_=pp), writes=["ppt"], stream="c1")
    S.op("sp", lambda e: e.dma_start(out=gbc[:, :, :], in_=gn.partition_broadcast(128)), writes=["gbc"], stream="c2")
    S.op("sp", lambda e: e.dma_start(out=vgbc[:, :, :], in_=vgb.partition_broadcast(128)), writes=["vgbc"], stream="c3")
    S.op("pool", lambda e: e.dma_start(out=w2b[:, :], in_=w2aug), writes=["w2b"], stream="c4")
    S.op("pool", lambda e: e.dma_start(out=a2b[:, :], in_=a2aug), writes=["a2b"], stream="c5")
    S.op("sp", lambda e: e.dma_start(out=wnat[:, :, :], in_=wsd.rearrange("g t s -> t g s")), writes=["wnat"], stream="c6")
    S.op("sp", lambda e: e.dma_start(out=bsf[:, :], in_=bsd), writes=["bsf"], stream="c7")
    S.op("dve", lambda e: e.tensor_copy(out=identb[:, :], in_=cval(O_ID, 128)), reads=["cst"], writes=["identb"])
    S.op("dve", lambda e: e.tensor_copy(out=bones[:, :], in_=cval(O_BO, 128)), reads=["cst"], writes=["bones"])
    S.op("dve", lambda e: e.tensor_scalar(out=pc[:, 0:16], in0=ppt[:, 64:80], scalar1=0.5, scalar2=None, op0=ALU.mult), reads=["ppt"], writes=["pc"])
    S.op("dve", lambda e: e.tensor_scalar(out=pc[:, 16:32], in0=ppt[:, 64:80], scalar1=-0.5, scalar2=1.0, op0=ALU.mult, op1=ALU.add), reads=["ppt"], writes=["pc"])
    S.op("dve", lambda e: e.tensor_scalar(out=pc[:, 32:64], in0=ppt[:, 80:112], scalar1=0.5, scalar2=None, op0=ALU.mult), reads=["ppt"], writes=["pc"])
    S.op("dve", lambda e: e.tensor_scalar(out=pc[:, 64:80], in0=ppt[:, 112:128], scalar1=0.5, scalar2=None, op0=ALU.mult), reads=["ppt"], writes=["pc"])
    for j in range(16):
        S.op("dve", lambda e, j=j: e.tensor_scalar(out=bdrk[:, j, :], in0=cval(O_BO, 128), scalar1=pc[:, 64 + j:65 + j], scalar2=None, op0=ALU.mult),
             reads=["cst", "pc"], writes=["bdrk"])
    S.op("pool", lambda e: e.memset(twd[:, :], 1.0), writes=["twd"])
    S.op("pool", lambda e: e.memset(epsk[:, 0:1], 1e-24), writes=["epsk"])
    S.op("pool", lambda e: e.memset(epsk[:, 1:2], 64e-5), writes=["epsk"])
    S.op("pool", lambda e: e.memset(adg[:, :], 1.0), writes=["adg"])
    S.op("pool", lambda e: e.memset(onesr[:, :], 1.0), writes=["onesr"])
    S.op("pool", lambda e: e.memset(carryP[:, :, :], 0.0), writes=["carryP"])
    for j in range(16):
        S.op("pool", lambda e, j=j: e.memset(SPb[:, 0, j, :], 0.0), writes=[("SPb", j, 0)])
    S.op("pool", lambda e: e.memset(bdt[:, :, :], 0.0), writes=["bdt"])
    for g in range(8):
        S.op("pe", lambda e, g=g: e.transpose(out=(pj[0] if g < 4 else pj[1])[:, (g % 4) * 128:(g % 4 + 1) * 128], in_=wnat[:, g, :], identity=cval(O_ID, 128)),
             reads=["wnat", "cst"], writes=[("pj", 0 if g < 4 else 1)])
    for hh in range(2):
        S.op("dve", lambda e, hh=hh: e.scalar_tensor_tensor(out=wmTs[:, hh * 4:hh * 4 + 4, :], in0=pj[hh][:, :].rearrange("p (g t) -> p g t", t=128), scalar=0.5,
                                                            in1=cval(O_TRIU, 128).unsqueeze(1).to_broadcast([128, 4, 128]), op0=ALU.mult, op1=ALU.mult),
             reads=[("pj", hh), "cst"], writes=["wmTs"])
    S.op("act", lambda e: e.activation(out=wmT[:, :, :], in_=wmTs[:, :, :], func=AF.Copy), reads=["wmTs"], writes=["wmT"])
    for b in range(16):
        S.op("sp", lambda e, b=b: e.dma_start(out=bdt[4 * b:4 * b + 4, :, 4 * b:4 * b + 4], in_=wmTs[0:4, :, 0:4]), reads=["wmTs"], writes=["bdt"], stream="c8")
    S.op("act", lambda e: e.activation(out=bdtb[:, :, :], in_=bdt[:, :, :], func=AF.Copy), reads=["bdt"], writes=["bdtb"])
    S.op("act", lambda e: e.activation(out=bsP[:, :], in_=bsf[:, :], func=AF.Copy, scale=0.5), reads=["bsf"], writes=["bsP"])
    S.op("dve", lambda e: e.tensor_copy(out=bsS[:, :, :].rearrange("p g (b t) -> p g b t", t=4),
                                        in_=bsP[:, :].rearrange("p (g t) -> p g t", t=128)[:, :, 0:4].unsqueeze(2).to_broadcast([1, 8, 16, 4])),
         reads=["bsP"], writes=["bsS"])

    S.fence()

    def rmsnorm_T(kind, nsub, RP, layer):
        for s in range(nsub):
            S.op("act", lambda e, s=s: e.activation(out=hb[0:RP, :], in_=xres[0:RP, s, :], func=AF.Square, accum_out=stat[0:RP, s:s + 1]),
                 reads=[("xres", s)], writes=["hb", ("stat", s)])
            S.op("dve", lambda e, s=s: e.tensor_scalar(out=stat[0:RP, 8 + s:9 + s], in0=stat[0:RP, s:s + 1], scalar1=1.0 / D, scalar2=1e-6, op0=ALU.mult, op1=ALU.add),
                 reads=[("stat", s)], writes=[("stat", 8 + s)])
            S.op("act", lambda e, s=s: e.activation(out=stat[0:RP, 16 + s:17 + s], in_=stat[0:RP, 8 + s:9 + s], func=AF.Sqrt),
                 reads=[("stat", 8 + s)], writes=[("stat", 16 + s)])
            S.op("dve", lambda e, s=s: e.reciprocal(out=stat[0:RP, 24 + s:25 + s], in_=stat[0:RP, 16 + s:17 + s]),
                 reads=[("stat", 16 + s)], writes=[("stat", 24 + s)])
            S.op("dve", lambda e, s=s: e.scalar_tensor_tensor(out=hb[0:RP, :], in0=xres[0:RP, s, :], scalar=stat[0:RP, 24 + s:25 + s], in1=gbc[0:RP, layer, :],
                                                              op0=ALU.mult, op1=ALU.mult),
                 reads=[("xres", s), ("stat", 24 + s), "gbc"], writes=["hb"])
            for k in range(8):
                S.op("pe", lambda e, k=k: e.transpose(out=ptr[:, k * 128:k * 128 + RP], in_=hb[0:RP, k * 128:(k + 1) * 128], identity=identb[0:RP, 0:RP]),
                     reads=["hb", "identb"], writes=["ptr"])
            S.op("act", lambda e, s=s: e.activation(out=hT[:, :, s * 128:s * 128 + RP], in_=ptr[:, :].rearrange("p (k t) -> p k t", t=128)[:, :, 0:RP], func=AF.Copy),
                 reads=["ptr"], writes=["hT"])

    def out_proj(wsrc, nsub, RP, NT):
        def body(n):
            slots = [load_group(wsrc[kh * 1024:(kh + 1) * 1024, n * 512:(n + 1) * 512], 512) for kh in range(2)]
            for s in range(nsub):
                pb, pu = nextbank()
                for kt in range(16):
                    S.op("pe", lambda e, kt=kt, s=s, pb=pb: e.matmul(pb[0:RP, :], lhsT=ofm[:, kt, s * 128:s * 128 + RP], rhs=wg[slots[kt // 8]][:, kt % 8, :],
                                                                     start=(kt == 0), stop=(kt == 15)),
                         reads=["ofm", ("wg", slots[kt // 8])], writes=[pu])
                S.op("dve", lambda e, s=s, pb=pb, n=n: e.tensor_tensor(out=xres[0:RP, s, n * 512:(n + 1) * 512], in0=pb[0:RP, :], in1=xres[0:RP, s, n * 512:(n + 1) * 512], op=ALU.add),
                     reads=[pu, ("xres", s)], writes=[("xres", s)])
        for n in range(2):
            body(n)

    def layer0(kind, nsub, RP, NT, nseq, T):
        samp = kind == "S"
        cin = sinT if samp else carryP
        cout = soutT if samp else carryP
        cin_u = "sinT" if samp else "carryP"
        cout_u = "soutT" if samp else "carryP"
        nb = nseq
        o_m2 = O_M2S if samp else O_M2P
        o_ml = O_MLS if samp else O_MLP
        o_rst = O_RSTS if samp else O_RSTP
        nch = NT // 64
        B = carve_l0(samp)
        sh, xm, lw, Lc, av, sq, kkn, k2, bbv = B.sh, B.xm, B.lw, B.Lc, B.av, B.sq, B.kkn, B.k2, B.bbv
        Pinv, Pm, yTM, Wf, tmpS, rkb = B.Pinv, B.Pm, B.yTM, B.Wf, B.tmpS, B.rkb
        Yb, Zb, Gb, Ub, ynb, shl, xml, dl_ = B.Yb, B.Zb, B.Gb, B.Ub, B.ynb, B.shl, B.xml, B.dl
        if samp:
            Ssf, Snat, Ssb, ATm, RTm, Um, Vm = B.Ssf, B.Snat, B.Ssb, B.ATm, B.RTm, B.Um, B.Vm

        def v3(ap, t):
            return ap.rearrange("p (b t) -> p b t", t=t)

        mark("l0_start")
        rmsnorm_T(kind, nsub, RP, 0)
        mark("l0_lora")
        slot = load_group(win0[:, 0:128], 128)
        for i in range(2):
            for k in range(8):
                S.op("pe", lambda e, i=i, k=k: e.matmul(pj[i][0:64, 0:NT], lhsT=wg[slot][:, k, i * 64:(i + 1) * 64], rhs=hT[:, k, 0:NT], start=(k == 0), stop=(k == 7)),
                     reads=["hT", ("wg", slot)], writes=[("pj", i)])
            shv = v3(shl[:, i, 0:nb * (T + 1)], T + 1)
            S.op("act", lambda e, i=i, shv=shv: e.activation(out=shv[:, :, 0:1], in_=cin[0:64, 48 + i, 0:nb].unsqueeze(2), func=AF.Copy), reads=[cin_u], writes=["shl"])
            S.op("act", lambda e, i=i, shv=shv: e.activation(out=shv[:, :, 1:T + 1], in_=v3(pj[i][0:64, 0:NT], T), func=AF.Copy), reads=[("pj", i)], writes=["shl"])
            S.op("act", lambda e, i=i, shv=shv: e.activation(out=cout[0:64, 48 + i, 0:nb].unsqueeze(2), in_=shv[:, :, T:T + 1], func=AF.Copy), reads=["shl"], writes=[cout_u])
            S.op("dve", lambda e, i=i, shv=shv: e.tensor_tensor(out=v3(dl_[:, i, 0:NT], T), in0=shv[:, :, 0:T], in1=shv[:, :, 1:T + 1], op=ALU.subtract), reads=["shl"], writes=["dl"])
            S.op("dve", lambda e, i=i, shv=shv: e.scalar_tensor_tensor(out=v3(xml[:, i, 0:NT], T), in0=v3(dl_[:, i, 0:NT], T), scalar=ppt[0:64, 128 + i:129 + i], in1=shv[:, :, 1:T + 1],
                                                                       op0=ALU.mult, op1=ALU.add), reads=["dl", "shl", "ppt"], writes=["xml"])
        S.op("act", lambda e: e.activation(out=twd[0:64, 0:NT], in_=xml[:, 0, 0:NT], func=AF.Tanh), reads=["xml"], writes=["twd"])
        S.op("act", lambda e: e.activation(out=adg[0:64, 0:NT], in_=xml[:, 1, 0:NT], func=AF.Copy), reads=["xml"], writes=["adg"])
        S.fence()

        def stageABC(j):
            par = j % 2
            gz, bon, Pt, AR, BK, Vb = B.gz[par], B.bon[par], B.Pt[par], B.AR[par], B.BK[par], B.Vb[par]
            BT, KT, VT, XA, XB, Wb = B.BT[par], B.KT[par], B.VT[par], B.XA[par], B.XB[par], B.Wb[par]
            XAv = XA[:, 0:nch * 128].rearrange("p (c n) -> p c n", n=128)
            XBv = XB[:, 0:nch * 128].rearrange("p (c n) -> p c n", n=128)
            ARv = AR[:, 0:nch * 128].rearrange("p (c two t) -> p c two t", two=2, t=64)
            BKv = BK[:, 0:nch * 128].rearrange("p (c two t) -> p c two t", two=2, t=64)

            def c3(ap):
                return ap.rearrange("p (c t) -> p c t", t=64)

            mark("l0_j%d" % j)
            if j + 1 < 16:
                gslots[j + 1] = load_group(win0[:, 128 + (j + 1) * 512:128 + (j + 2) * 512], 512)
            slot = gslots[j]
            jsl = slice(j, j + 1)
            for i in range(4):
                pb = pj[i % 2]
                for k in range(8):
                    S.op("pe", lambda e, i=i, k=k, pb=pb: e.matmul(pb[:, 0:NT], lhsT=wg[slot][:, k, i * 128:(i + 1) * 128], rhs=hT[:, k, 0:NT], start=(k == 0), stop=(k == 7)),
                         reads=["hT", ("wg", slot)], writes=[("pj", i % 2)])
                if i < 3:
                    cidx = i * 16 + j
                    shv = v3(sh[:, i * 520:i * 520 + nb * (T + 1)], T + 1)
                    S.op("act", lambda e, shv=shv, cidx=cidx: e.activation(out=shv[:, :, 0:1], in_=cin[:, cidx, 0:nb].unsqueeze(2), func=AF.Copy), reads=[cin_u], writes=[("sh", i)])
                    S.op("act", lambda e, shv=shv, pb=pb: e.activation(out=shv[:, :, 1:T + 1], in_=v3(pb[:, 0:NT], T), func=AF.Copy), reads=[("pj", i % 2)], writes=[("sh", i)])
                    S.op("act", lambda e, shv=shv, cidx=cidx: e.activation(out=cout[:, cidx, 0:nb].unsqueeze(2), in_=shv[:, :, T:T + 1], func=AF.Copy), reads=[("sh", i)], writes=[cout_u])
                    S.op("dve", lambda e, shv=shv: e.tensor_tensor(out=v3(Pm[:, 0:NT], T), in0=shv[:, :, 0:T], in1=shv[:, :, 1:T + 1], op=ALU.subtract), reads=[("sh", i)], writes=["Pm"])
                    S.op("dve", lambda e, shv=shv, i=i: e.scalar_tensor_tensor(out=v3(xm[:, i * 512:i * 512 + NT], T), in0=v3(Pm[:, 0:NT], T), scalar=ppt[:, i * 16 + j:i * 16 + j + 1],
                                                                               in1=shv[:, :, 1:T + 1], op0=ALU.mult, op1=ALU.add), reads=["Pm", ("sh", i), "ppt"], writes=[("xm", i)])
                else:
                    S.op("act", lambda e, pb=pb: e.activation(out=Lc[:, 0:NT], in_=pb[:, 0:NT], func=AF.Tanh, scale=0.5), reads=[("pj", i % 2)], writes=["Lc"])
                    S.op("dve", lambda e, pb=pb: e.scalar_tensor_tensor(out=gz[:, 0:NT], in0=Lc[:, 0:NT], scalar=1.0, in1=pb[:, 0:NT], op0=ALU.add, op1=ALU.mult),
                         reads=["Lc", ("pj", i % 2)], writes=[("gz", par)])
            mark("l0_prep%d" % j)
            r_ = xm[:, 0:NT]
            k_ = xm[:, 512:512 + NT]
            v_ = xm[:, 1024:1024 + NT]
            S.op("pe", lambda e: e.matmul(pm[:, 0:NT], lhsT=w2b[:, j * 128:(j + 1) * 128], rhs=twd[:, 0:NT], start=True, stop=True), reads=["w2b", "twd"], writes=["pm"])
            S.op("act", lambda e: e.activation(out=lw[:, 0:NT], in_=pm[:, 0:NT], func=AF.Tanh, scale=0.5), reads=["pm"], writes=["lw"])
            S.op("dve", lambda e: e.tensor_scalar(out=lw[:, 0:NT], in0=lw[:, 0:NT], scalar1=C_LW, scalar2=C_LW, op0=ALU.mult, op1=ALU.add), reads=["lw"], writes=["lw"])
            S.op("pe", lambda e: e.matmul(pm[:, 0:NT], lhsT=a2b[:, j * 128:(j + 1) * 128], rhs=adg[:, 0:NT], start=True, stop=True), reads=["a2b", "adg"], writes=["pm"])
            S.op("act", lambda e: e.activation(out=av[:, 0:NT], in_=pm[:, 0:NT], func=AF.Tanh, scale=0.5), reads=["pm"], writes=["av"])
            S.op("dve", lambda e: e.tensor_scalar(out=k2[:, 0:NT], in0=av[:, 0:NT], scalar1=pc[:, j:j + 1], scalar2=pc[:, 16 + j:17 + j], op0=ALU.mult, op1=ALU.add),
                 reads=["av", "pc"], writes=["k2"])
            S.op("dve", lambda e: e.tensor_tensor(out=k2[:, 0:NT], in0=k2[:, 0:NT], in1=k_, op=ALU.mult), reads=["k2", ("xm", 1)], writes=["k2"])
            S.op("dve", lambda e: e.tensor_scalar(out=av[:, 0:NT], in0=av[:, 0:NT], scalar1=0.5, scalar2=0.5, op0=ALU.mult, op1=ALU.add), reads=["av"], writes=["av"])
            S.op("act", lambda e: e.activation(out=sq[:, 0:NT], in_=k_, func=AF.Square, scale=ppt[:, 48 + j:49 + j]), reads=[("xm", 1), "ppt"], writes=["sq"])
            S.op("pe", lambda e: e.matmul(pm[:, 0:NT], lhsT=cval(O_BO, 128), rhs=sq[:, 0:NT], start=True, stop=True), reads=["cst", "sq"], writes=["pm"])
            S.op("act", lambda e: e.activation(out=sq[:, 0:NT], in_=pm[:, 0:NT], func=AF.Ln, bias=epsk[:, 0:1]), reads=["pm", "epsk"], writes=["sq"])
            S.op("act", lambda e: e.activation(out=sq[:, 0:NT], in_=sq[:, 0:NT], func=AF.Exp, scale=-0.5), reads=["sq"], writes=["sq"])
            S.op("dve", lambda e: e.scalar_tensor_tensor(out=kkn[:, 0:NT], in0=k_, scalar=ppt[:, 48 + j:49 + j], in1=sq[:, 0:NT], op0=ALU.mult, op1=ALU.mult),
                 reads=[("xm", 1), "ppt", "sq"], writes=["kkn"])
            S.op("dve", lambda e: e.tensor_tensor(out=bbv[:, 0:NT], in0=kkn[:, 0:NT], in1=av[:, 0:NT], op=ALU.mult), reads=["kkn", "av"], writes=["bbv"])
            S.op("dve", lambda e: e.tensor_tensor(out=rkb[:, 0:NT], in0=r_, in1=k2[:, 0:NT], op=ALU.mult), reads=[("xm", 0), "k2"], writes=["rkb"])
            S.op("pe", lambda e: e.matmul(pm[:, 0:NT], lhsT=bdrk[:, j, :], rhs=rkb[:, 0:NT], start=True, stop=True), reads=["bdrk", "rkb"], writes=["pm"])
            S.op("dve", lambda e: e.tensor_tensor(out=bon[:, 0:NT], in0=pm[:, 0:NT], in1=v_, op=ALU.mult), reads=["pm", ("xm", 2)], writes=[("bon", par)])
            S.op("dve", lambda e: e.tensor_tensor_scan(out=Lc[:, 0:NT], data0=cval(o_rst, NT), data1=lw[:, 0:NT], initial=0.0, op0=ALU.mult, op1=ALU.add),
                 reads=["cst", "lw"], writes=["Lc"])
            S.op("act", lambda e: e.activation(out=Pt[:, 0:NT], in_=Lc[:, 0:NT], func=AF.Exp), reads=["Lc"], writes=[("Pt", par)])
            S.op("act", lambda e: e.activation(out=Pinv[:, 0:NT], in_=Lc[:, 0:NT], func=AF.Exp, scale=-1.0), reads=["Lc"], writes=["Pinv"])
            S.op("dve", lambda e: e.tensor_tensor(out=Pm[:, 0:NT], in0=Lc[:, 0:NT], in1=lw[:, 0:NT], op=ALU.subtract), reads=["Lc", "lw"], writes=["Pm"])
            S.op("act", lambda e: e.activation(out=Pm[:, 0:NT], in_=Pm[:, 0:NT], func=AF.Exp), reads=["Pm"], writes=["Pm"])
            S.op("dve", lambda e: e.scalar_tensor_tensor(out=ARv[:, :, 0, :], in0=c3(kkn[:, 0:NT]), scalar=-1.0, in1=c3(Pm[:, 0:NT]), op0=ALU.mult, op1=ALU.mult),
                 reads=["kkn", "Pm"], writes=[("AR", par)])
            S.op("dve", lambda e: e.tensor_tensor(out=ARv[:, :, 1, :], in0=c3(r_), in1=c3(Pt[:, 0:NT]), op=ALU.mult), reads=[("xm", 0), ("Pt", par)], writes=[("AR", par)])
            S.op("dve", lambda e: e.tensor_tensor(out=BKv[:, :, 0, :], in0=c3(bbv[:, 0:NT]), in1=c3(Pinv[:, 0:NT]), op=ALU.mult), reads=["bbv", "Pinv"], writes=[("BK", par)])
            S.op("dve", lambda e: e.tensor_tensor(out=BKv[:, :, 1, :], in0=c3(k2[:, 0:NT]), in1=c3(Pinv[:, 0:NT]), op=ALU.mult), reads=["k2", "Pinv"], writes=[("BK", par)])
            S.op("act", lambda e: e.activation(out=Vb[:, 0:NT], in_=v_, func=AF.Copy), reads=[("xm", 2)], writes=[("Vb", par)])
            mark("l0_tm%d" % j)
            tmb = ((pm, "pm"), (pY, "pY"), (pZ, "pZ"))
            tmg = ((lambda c: BKv[:, c, 0, :], BT, ("BT", par)), (lambda c: BKv[:, c, 1, :], KT, ("KT", par)), (lambda c: Vb[:, c * 64:(c + 1) * 64], VT, ("VT", par)))
            for gi, (src, dst, uu) in enumerate(tmg):
                pb, pu = tmb[gi]
                for c in range(nch):
                    for h in range(2):
                        hs = slice(64 * h, 64 * h + 64)
                        S.op("pe", lambda e, c=c, hs=hs, h=h, src=src, pb=pb: e.matmul(pb[hs, c * 64:(c + 1) * 64], lhsT=src(c)[hs, :], rhs=identb[hs, hs], start=True, stop=True,
                                                                                    tile_position=(64 * h, 64 * h)),
                             reads=[("BK", par), ("Vb", par), "identb"], writes=[pu])
            for gi, (src, dst, uu) in enumerate(tmg):
                pb, pu = tmb[gi]
                S.op("act", lambda e, dst=dst, pb=pb: e.activation(out=dst[:, 0:NT], in_=pb[:, 0:NT], func=AF.Copy), reads=[pu], writes=[uu])
            mark("l0_mat%d" % j)
            xgb = [(pm, "pm"), (pY, "pY"), (pZ, "pZ"), (pm, "pm")]
            xgs = []
            for (which, dstv, uu) in ((0, XAv, ("XA", par)), (1, XBv, ("XB", par))):
                for c0_ in range(0, nch, 4):
                    xgs.append((which, dstv, uu, c0_, min(4, nch - c0_)))
            def xg_ev(gi):
                which, dstv, uu, c0_, cn = xgs[gi]
                pb, pu = xgb[gi]
                S.op("dve", lambda e, c0_=c0_, cn=cn, dstv=dstv, pb=pb: e.tensor_tensor(out=dstv[:, c0_:c0_ + cn, :], in0=pb[:, 0:cn * 128].rearrange("p (c n) -> p c n", n=128),
                                                                                       in1=cval(o_m2, 128).unsqueeze(1).to_broadcast([128, cn, 128]), op=ALU.mult),
                     reads=[pu, "cst"], writes=[uu])

            for gi, (which, dstv, uu, c0_, cn) in enumerate(xgs):
                pb, pu = xgb[gi]
                if gi == 3:
                    xg_ev(0)
                for c in range(c0_, c0_ + cn):
                    for h in range(2):
                        hs = slice(64 * h, 64 * h + 64)
                        S.op("pe", lambda e, c=c, hs=hs, h=h, which=which, c0_=c0_, pb=pb: e.matmul(pb[hs, (c - c0_) * 128:(c - c0_ + 1) * 128], lhsT=BKv[hs, c, which, :],
                                                                                                    rhs=AR[hs, c * 128:(c + 1) * 128], start=True, stop=True, tile_position=(64 * h, 64 * h)),
                             reads=[("BK", par), ("AR", par)], writes=[pu])
            for gi in range(len(xgs)):
                if gi == 0 and len(xgs) > 3:
                    continue
                xg_ev(gi)
            for c in range(nch):
                for h in range(2):
                    hs = slice(64 * h, 64 * h + 64)
                    S.op("pe", lambda e, c=c, hs=hs, h=h: e.matmul(pZ[hs, c * 64:(c + 1) * 64], lhsT=ARv[hs, c, 0, :], rhs=BKv[hs, c, 0, :], start=True, stop=True,
                                                                   tile_position=(64 * h, 64 * h)), reads=[("AR", par), ("BK", par)], writes=["pZ"])
            S.op("dve", lambda e: e.tensor_tensor(out=c3(Zb[0][:, 0:NT]), in0=c3(pZ[:, 0:NT]), in1=cval(o_ml, 64).unsqueeze(1).to_broadcast([128, nch, 64]), op=ALU.mult),
                 reads=["pZ", "cst"], writes=[("Zb", 0)])
            S.op("act", lambda e: e.activation(out=c3(Yb[0][:, 0:NT]), in_=XAv[:, :, 0:64], func=AF.Copy), reads=[("XA", par)], writes=[("Yb", 0)])
            S.op("dve", lambda e: e.tensor_tensor(out=c3(Wf[:, 0:NT]), in0=XAv[:, :, 0:64], in1=cval(O_ID, 64).unsqueeze(1).to_broadcast([128, nch, 64]), op=ALU.add),
                 reads=[("XA", par), "cst"], writes=["Wf"])
            S.op("dve", lambda e: e.tensor_tensor(out=c3(Wf[64:128, 0:NT]), in0=XAv[64:128, :, 0:64], in1=cst[64:128, O_ID + 64:O_ID + 128].unsqueeze(1).to_broadcast([64, nch, 64]), op=ALU.add),
                 reads=[("XA", par), "cst", "Wf"], writes=["Wf"])
            S.op("act", lambda e: e.activation(out=Wb[:, 0:NT], in_=Wf[:, 0:NT], func=AF.Copy), reads=["Wf"], writes=[("Wb", par)])
            nlev = 1 if samp else 5

            def yz_mm(lv):
                a_ = lv % 2
                last = lv == nlev - 1
                for c in range(nch):
                    for h in range(2):
                        hs = slice(64 * h, 64 * h + 64)
                        cs = slice(c * 64, (c + 1) * 64)
                        if not last:
                            S.op("pe", lambda e, hs=hs, cs=cs, h=h, a_=a_: e.matmul(pY[hs, cs], lhsT=Zb[a_][hs, cs], rhs=Yb[a_][hs, cs], start=True, stop=True, tile_position=(64 * h, 64 * h)),
                                 reads=[("Zb", a_), ("Yb", a_)], writes=["pY"])
                        S.op("pe", lambda e, hs=hs, cs=cs, h=h, a_=a_: e.matmul(pZ[hs, cs], lhsT=Yb[a_][hs, cs], rhs=Zb[a_][hs, cs], start=True, stop=True, tile_position=(64 * h, 64 * h)),
                             reads=[("Zb", a_), ("Yb", a_)], writes=["pZ"])

            def yz_ev(lv):
                b_ = (lv + 1) % 2
                if lv != nlev - 1:
                    S.op("act", lambda e, b_=b_: e.activation(out=Yb[b_][:, 0:NT], in_=pY[:, 0:NT], func=AF.Copy), reads=["pY"], writes=[("Yb", b_)])
                S.op("dve", lambda e, b_=b_: e.tensor_copy(out=Zb[b_][:, 0:NT], in_=pZ[:, 0:NT]), reads=["pZ"], writes=[("Zb", b_)])

            def w_mm(lv):
                b_ = (lv + 1) % 2
                for c in range(nch):
                    for h in range(2):
                        hs = slice(64 * h, 64 * h + 64)
                        cs = slice(c * 64, (c + 1) * 64)
                        S.op("pe", lambda e, hs=hs, cs=cs, h=h, b_=b_: e.matmul(pm[hs, cs], lhsT=Zb[b_][hs, cs], rhs=Wb[hs, cs], start=True, stop=True, tile_position=(64 * h, 64 * h)),
                             reads=[("Zb", b_), ("Wb", par)], writes=["pm"])

            def w_ev(lv):
                S.op("dve", lambda e: e.tensor_tensor(out=Wf[:, 0:NT], in0=pm[:, 0:NT], in1=Wf[:, 0:NT], op=ALU.add), reads=["pm", "Wf"], writes=["Wf"])
                S.op("act", lambda e: e.activation(out=Wb[:, 0:NT], in_=Wf[:, 0:NT], func=AF.Copy), reads=["Wf"], writes=[("Wb", par)])

            for lv in range(nlev):
                yz_mm(lv)
                if lv >= 1:
                    w_mm(lv - 1)
                yz_ev(lv)
                if lv >= 1:
                    w_ev(lv - 1)
            w_mm(nlev - 1)
            w_ev(nlev - 1)

        def stageDE(j):
            par = j % 2
            gz, bon, Pt, AR, BK, Vb = B.gz[par], B.bon[par], B.Pt[par], B.AR[par], B.BK[par], B.Vb[par]
            BT, KT, VT, XA, XB, Wb = B.BT[par], B.KT[par], B.VT[par], B.XA[par], B.XB[par], B.Wb[par]
            XAv = XA[:, 0:nch * 128].rearrange("p (c n) -> p c n", n=128)
            XBv = XB[:, 0:nch * 128].rearrange("p (c n) -> p c n", n=128)
            ARv = AR[:, 0:nch * 128].rearrange("p (c two t) -> p c two t", two=2, t=64)
            BKv = BK[:, 0:nch * 128].rearrange("p (c two t) -> p c two t", two=2, t=64)

            def c3(ap):
                return ap.rearrange("p (c t) -> p c t", t=64)

            mark("l0_chain%d" % j)
            G_ = ptrf[:, 0:64]
            U_ = ptrf[:, 64:128]
            Y_ = pch[:, 0:64]
            if not samp:
                D_ = pM[:, 0:64]
                for c in range(nch):
                    cs = slice(c * 64, (c + 1) * 64)
                    p0, p1 = c % 2, (c + 1) % 2
                    for h in range(2):
                        hs = slice(64 * h, 64 * h + 64)
                        tp = (64 * h, 64 * h)
                        S.op("pe", lambda e, hs=hs, tp=tp, c=c, cs=cs: e.matmul(G_[hs, :], lhsT=XBv[hs, c, 0:64], rhs=VT[hs, cs], start=True, stop=False, tile_position=tp),
                             reads=[("XB", par), ("VT", par)], writes=["ptr"])
                    for h in range(2):
                        hs = slice(64 * h, 64 * h + 64)
                        tp = (64 * h, 64 * h)
                        S.op("pe", lambda e, hs=hs, tp=tp, c=c, p0=p0: e.matmul(G_[hs, :], lhsT=ARv[hs, c, 0, :], rhs=SPb[hs, p0, j, :], start=False, stop=True, tile_position=tp),
                             reads=[("AR", par), ("SPb", j, p0)], writes=["ptr"])
                    S.op("act", lambda e: e.activation(out=Gb[:, 0:64], in_=G_, func=AF.Copy), reads=["ptr"], writes=["Gb"])
                    for h in range(2):
                        hs = slice(64 * h, 64 * h + 64)
                        tp = (64 * h, 64 * h)
                        S.op("pe", lambda e, hs=hs, tp=tp, c=c, cs=cs, p0=p0: e.matmul(pch[hs, cs], lhsT=ARv[hs, c, 1, :], rhs=SPb[hs, p0, j, :], start=True, stop=False, tile_position=tp),
                             reads=[("AR", par), ("SPb", j, p0)], writes=["pch"])
                        S.op("pe", lambda e, hs=hs, tp=tp, c=c, cs=cs: e.matmul(pch[hs, cs], lhsT=XBv[hs, c, 64:128], rhs=VT[hs, cs], start=False, stop=False, tile_position=tp),
                             reads=[("XB", par), ("VT", par)], writes=["pch"])
                    for h in range(2):
                        hs = slice(64 * h, 64 * h + 64)
                        tp = (64 * h, 64 * h)
                        S.op("pe", lambda e, hs=hs, tp=tp, cs=cs: e.matmul(U_[hs, :], lhsT=Wb[hs, cs], rhs=Gb[hs, 0:64], start=True, stop=True, tile_position=tp),
                             reads=[("Wb", par), "Gb"], writes=["ptr"])
                    S.op("act", lambda e: e.activation(out=Ub[:, 0:64], in_=U_, func=AF.Copy), reads=["ptr"], writes=["Ub"])
                    for h in range(2):
                        hs = slice(64 * h, 64 * h + 64)
                        tp = (64 * h, 64 * h)
                        S.op("pe", lambda e, hs=hs, tp=tp, p0=p0: e.matmul(D_[hs, :], lhsT=identb[hs, hs], rhs=SPb[hs, p0, j, :], start=True, stop=False, tile_position=tp),
                             reads=["identb", ("SPb", j, p0)], writes=["pM"])
                        S.op("pe", lambda e, hs=hs, tp=tp, cs=cs: e.matmul(D_[hs, :], lhsT=KT[hs, cs], rhs=VT[hs, cs], start=False, stop=False, tile_position=tp),
                             reads=[("KT", par), ("VT", par)], writes=["pM"])
                    for h in range(2):
                        hs = slice(64 * h, 64 * h + 64)
                        tp = (64 * h, 64 * h)
                        S.op("pe", lambda e, hs=hs, tp=tp, cs=cs: e.matmul(D_[hs, :], lhsT=BT[hs, cs], rhs=Ub[hs, 0:64], start=False, stop=True, tile_position=tp),
                             reads=[("BT", par), "Ub"], writes=["pM"])
                    pcol = Pt[:, c * 64 + 63:c * 64 + 64]
                    S.op("act", lambda e, pcol=pcol, p1=p1: e.activation(out=SPb[:, p1, j, :], in_=D_, func=AF.Copy, scale=pcol), reads=["pM", ("Pt", par)], writes=[("SPb", j, p1)])
                    for h in range(2):
                        hs = slice(64 * h, 64 * h + 64)
                        tp = (64 * h, 64 * h)
                        S.op("pe", lambda e, hs=hs, tp=tp, c=c, cs=cs: e.matmul(pch[hs, cs], lhsT=XAv[hs, c, 64:128], rhs=Ub[hs, 0:64], start=False, stop=True, tile_position=tp),
                             reads=[("XA", par), "Ub"], writes=["pch"])
                S.op("act", lambda e: e.activation(out=yTM[:, 0:NT], in_=pch[:, 0:NT], func=AF.Copy), reads=["pch"], writes=["yTM"])
            else:
                Sv = Ssf.rearrange("p (b v) -> p b v", v=64)
                Sbv = Ssb[:, 0:1024].rearrange("p (b v) -> p b v", v=64)
                Snv = Snat.rearrange("p (b v) -> p b v", v=64)
                for b in range(16):
                    S.op("sp", lambda e, b=b: e.dma_start(out=Snv[:, b, :], in_=swkv[b, 2 * j:2 * j + 2, :, :].rearrange("h v k -> (h v) k")), writes=["Snat"], stream="sin")
                for half in range(2):
                    pb = pM
                    for b in range(8):
                        bb_ = half * 8 + b
                        for h in range(2):
                            hs = slice(64 * h, 64 * h + 64)
                            S.op("pe", lambda e, hs=hs, h=h, b=b, bb_=bb_, pb=pb: e.matmul(pb[hs, b * 64:(b + 1) * 64], lhsT=Snv[hs, bb_, :], rhs=cst[hs, O_ID + 64 * h:O_ID + 64 * h + 64],
                                                                                           start=True, stop=True, tile_position=(64 * h, 64 * h)),
                                 reads=["Snat", "cst"], writes=["pM"])
                    S.op("dve", lambda e, half=half, pb=pb: e.tensor_copy(out=Ssf[:, half * 512:(half + 1) * 512], in_=pb[:, :]), reads=["pM"], writes=["Ssf"])
                    S.op("act", lambda e, half=half: e.activation(out=Ssb[:, half * 512:(half + 1) * 512], in_=Ssf[:, half * 512:(half + 1) * 512], func=AF.Copy), reads=["Ssf"], writes=["Ssb"])
                seqm = cst[:, O_SEQ:O_SEQ + 1024].rearrange("p (b t) -> p b t", t=64)
                rowm = cst[:, O_ROW:O_ROW + 16]
                ATv = ATm[:, 0:1024].rearrange("p (b t) -> p b t", t=64)
                RTv = RTm[:, 0:1024].rearrange("p (b t) -> p b t", t=64)
                S.op("dve", lambda e: e.tensor_tensor(out=ATv, in0=ARv[:, 0:1, 0, :].to_broadcast([128, 16, 64]), in1=seqm, op=ALU.mult), reads=[("AR", par), "cst"], writes=["ATm"])
                S.op("dve", lambda e: e.tensor_tensor(out=RTv, in0=ARv[:, 0:1, 1, :].to_broadcast([128, 16, 64]), in1=seqm, op=ALU.mult), reads=[("AR", par), "cst"], writes=["RTm"])
                for h in range(2):
                    hs = slice(64 * h, 64 * h + 64)
                    tp = (64 * h, 64 * h)
                    for b in range(16):
                        S.op("pe", lambda e, hs=hs, tp=tp, b=b: e.matmul(G_[hs, :], lhsT=ATv[hs, b, :], rhs=Sbv[hs, b, :], start=(b == 0), stop=False, tile_position=tp),
                             reads=["ATm", "Ssb"], writes=["ptr"])
                    S.op("pe", lambda e, hs=hs, tp=tp: e.matmul(G_[hs, :], lhsT=XBv[hs, 0, 0:64], rhs=VT[hs, 0:64], start=False, stop=True, tile_position=tp),
                         reads=[("XB", par), ("VT", par)], writes=["ptr"])
                S.op("act", lambda e: e.activation(out=Gb[:, 0:64], in_=G_, func=AF.Copy), reads=["ptr"], writes=["Gb"])
                for h in range(2):
                    hs = slice(64 * h, 64 * h + 64)
                    tp = (64 * h, 64 * h)
                    S.op("pe", lambda e, hs=hs, tp=tp: e.matmul(U_[hs, :], lhsT=Wb[hs, 0:64], rhs=Gb[hs, 0:64], start=True, stop=True, tile_position=tp), reads=[("Wb", par), "Gb"], writes=["ptr"])
                S.op("act", lambda e: e.activation(out=Ub[:, 0:64], in_=U_, func=AF.Copy), reads=["ptr"], writes=["Ub"])
                for h in range(2):
                    hs = slice(64 * h, 64 * h + 64)
                    tp = (64 * h, 64 * h)
                    for b in range(16):
                        S.op("pe", lambda e, hs=hs, tp=tp, b=b: e.matmul(Y_[hs, :], lhsT=RTv[hs, b, :], rhs=Sbv[hs, b, :], start=(b == 0), stop=False, tile_position=tp),
                             reads=["RTm", "Ssb"], writes=["pch"])
                    S.op("pe", lambda e, hs=hs, tp=tp: e.matmul(Y_[hs, :], lhsT=XAv[hs, 0, 64:128], rhs=Ub[hs, 0:64], start=False, stop=False, tile_position=tp), reads=[("XA", par), "Ub"], writes=["pch"])
                    S.op("pe", lambda e, hs=hs, tp=tp: e.matmul(Y_[hs, :], lhsT=XBv[hs, 0, 64:128], rhs=VT[hs, 0:64], start=False, stop=True, tile_position=tp), reads=[("XB", par), ("VT", par)], writes=["pch"])
                S.op("act", lambda e: e.activation(out=yTM[:, 0:64], in_=Y_, func=AF.Copy), reads=["pch"], writes=["yTM"])
                Umv = Um[:, 0:1024].rearrange("p (b v) -> p b v", v=64)
                Vmv = Vm[:, 0:1024].rearrange("p (b v) -> p b v", v=64)
                S.op("dve", lambda e: e.tensor_tensor(out=Umv, in0=Ub[:, 0:64].unsqueeze(1).to_broadcast([128, 16, 64]), in1=rowm.unsqueeze(2).to_broadcast([128, 16, 64]), op=ALU.mult),
                     reads=["Ub", "cst"], writes=["Um"])
                S.op("dve", lambda e: e.tensor_tensor(out=Vmv, in0=VT[:, 0:64].unsqueeze(1).to_broadcast([128, 16, 64]), in1=rowm.unsqueeze(2).to_broadcast([128, 16, 64]), op=ALU.mult),
                     reads=[("VT", par), "cst"], writes=["Vm"])
                P3 = Pt[:, 0:64].rearrange("p (b t) -> p b t", t=4)[:, :, 3:4]
                for half in range(2):
                    pb = pM
                    for h in range(2):
                        hs = slice(64 * h, 64 * h + 64)
                        tp = (64 * h, 64 * h)
                        S.op("pe", lambda e, hs=hs, tp=tp, half=half, pb=pb: e.matmul(pb[hs, :], lhsT=BT[hs, 0:64], rhs=Um[hs, half * 512:(half + 1) * 512], start=True, stop=False, tile_position=tp),
                             reads=[("BT", par), "Um"], writes=["pM"])
                        S.op("pe", lambda e, hs=hs, tp=tp, half=half, pb=pb: e.matmul(pb[hs, :], lhsT=KT[hs, 0:64], rhs=Vm[hs, half * 512:(half + 1) * 512], start=False, stop=True, tile_position=tp),
                             reads=[("KT", par), "Vm"], writes=["pM"])
                    hsl = slice(half * 512, (half + 1) * 512)
                    S.op("dve", lambda e, pb=pb, hsl=hsl: e.tensor_tensor(out=tmpS[:, hsl], in0=pb[:, :], in1=Ssf[:, hsl], op=ALU.add), reads=["pM", "Ssf"], writes=["tmpS"])
                    S.op("dve", lambda e, half=half, hsl=hsl: e.tensor_tensor(out=Ssf[:, hsl].rearrange("p (b v) -> p b v", v=64), in0=tmpS[:, hsl].rearrange("p (b v) -> p b v", v=64),
                                                                               in1=P3[:, half * 8:(half + 1) * 8, :].to_broadcast([128, 8, 64]), op=ALU.mult), reads=["tmpS", ("Pt", par)], writes=["Ssf"])
                for half in range(2):
                    pb = pM
                    for b in range(8):
                        bb_ = half * 8 + b
                        for h in range(2):
                            hs = slice(64 * h, 64 * h + 64)
                            S.op("pe", lambda e, hs=hs, h=h, b=b, bb_=bb_, pb=pb: e.matmul(pb[hs, b * 64:(b + 1) * 64], lhsT=Sv[hs, bb_, :], rhs=cst[hs, O_ID + 64 * h:O_ID + 64 * h + 64],
                                                                                           start=True, stop=True, tile_position=(64 * h, 64 * h)),
                                 reads=["Ssf", "cst"], writes=["pM"])
                    S.op("act", lambda e, half=half, pb=pb: e.activation(out=Snat[:, half * 512:(half + 1) * 512], in_=pb[:, :], func=AF.Copy), reads=["pM"], writes=["Snat"])
                for b in range(16):
                    S.op("sp", lambda e, b=b: e.dma_start(out=swkv_o[b, 2 * j:2 * j + 2, :, :].rearrange("h v k -> (h v) k"), in_=Snv[:, b, :]), reads=["Snat"], stream="out")

            mark("l0_gn%d" % j)
            yv = yTM[:, 0:NT].rearrange("p (c v) -> p c v", v=64)
            S.op("dve", lambda e: e.tensor_reduce(out=stat[:, 32:32 + nch], in_=yv, axis=AX.X, op=ALU.add), reads=["yTM"], writes=[("stat", 32)])
            S.op("act", lambda e: e.activation(out=tmpS[:, 0:NT], in_=yTM[:, 0:NT], func=AF.Square), reads=["yTM"], writes=["tmpS"])
            S.op("dve", lambda e: e.tensor_reduce(out=stat[:, 40:40 + nch], in_=tmpS[:, 0:NT].rearrange("p (c v) -> p c v", v=64), axis=AX.X, op=ALU.add), reads=["tmpS"], writes=[("stat", 40)])
            S.op("dve", lambda e: e.tensor_scalar(out=stat[:, 32:32 + nch], in0=stat[:, 32:32 + nch], scalar1=1.0 / 64, scalar2=None, op0=ALU.mult), reads=[("stat", 32)], writes=[("stat", 32)])
            S.op("dve", lambda e: e.tensor_tensor(out=stat[:, 48:48 + nch], in0=stat[:, 32:32 + nch], in1=stat[:, 32:32 + nch], op=ALU.mult), reads=[("stat", 32)], writes=[("stat", 48)])
            S.op("dve", lambda e: e.scalar_tensor_tensor(out=stat[:, 40:40 + nch], in0=stat[:, 40:40 + nch], scalar=1.0 / 64, in1=stat[:, 48:48 + nch], op0=ALU.mult, op1=ALU.subtract),
                 reads=[("stat", 40), ("stat", 48)], writes=[("stat", 40)])
            S.op("act", lambda e: e.activation(out=stat[:, 40:40 + nch], in_=stat[:, 40:40 + nch], func=AF.Ln, bias=epsk[:, 1:2]), reads=[("stat", 40), "epsk"], writes=[("stat", 40)])
            S.op("act", lambda e: e.activation(out=stat[:, 40:40 + nch], in_=stat[:, 40:40 + nch], func=AF.Exp, scale=-0.5), reads=[("stat", 40)], writes=[("stat", 40)])
            S.op("dve", lambda e: e.tensor_tensor(out=yv, in0=yv, in1=stat[:, 32:32 + nch].unsqueeze(2).to_broadcast([128, nch, 64]), op=ALU.subtract), reads=["yTM", ("stat", 32)], writes=["yTM"])
            S.op("dve", lambda e: e.tensor_tensor(out=ynb[:, 0:NT].rearrange("p (c v) -> p c v", v=64), in0=yv, in1=stat[:, 40:40 + nch].unsqueeze(2).to_broadcast([128, nch, 64]), op=ALU.mult),
                 reads=["yTM", ("stat", 40)], writes=["ynb"])
            for c in range(nch):
                for h in range(2):
                    hs = slice(64 * h, 64 * h + 64)
                    S.op("pe", lambda e, c=c, hs=hs, h=h: e.matmul(ptrf[hs, c * 64:(c + 1) * 64], lhsT=ynb[hs, c * 64:(c + 1) * 64], rhs=identb[hs, hs], start=True, stop=True, tile_position=(64 * h, 64 * h)),
                         reads=["ynb", "identb"], writes=["ptr"])
            S.op("dve", lambda e: e.scalar_tensor_tensor(out=bon[:, 0:NT], in0=ptrf[:, 0:NT], scalar=pc[:, 32 + j:33 + j], in1=bon[:, 0:NT], op0=ALU.mult, op1=ALU.add),
                 reads=["ptr", "pc", ("bon", par)], writes=[("bon", par)])
            S.op("dve", lambda e: e.scalar_tensor_tensor(out=ofm[:, j, 0:NT], in0=bon[:, 0:NT], scalar=pc[:, 48 + j:49 + j], in1=gz[:, 0:NT], op0=ALU.add, op1=ALU.mult),
                 reads=[("bon", par), "pc", ("gz", par)], writes=["ofm"])
        gslots = {0: load_group(win0[:, 128:128 + 512], 512)}
        for _w in range(_DEBUG.get("warm", 0)):
            S.op("pe", lambda e: e.matmul(pch[:, :], lhsT=hT[:, 0, 0:128], rhs=hT[:, 1, 0:512], start=True, stop=True), reads=["hT"], writes=["pch"])

        def rec(fn, j):
            S.buf = []
            fn(j)
            L = S.buf
            S.buf = None
            return L

        def replay(L):
            for a_ in L:
                S.op(*a_)

        def merge(X, Y):
            if not Y:
                return X
            out = []
            nx, ny = len(X), len(Y)
            iy = 0
            for ix in range(nx):
                out.append(X[ix])
                tgt = (ix + 1) * ny // nx
                while iy < tgt:
                    out.append(Y[iy])
                    iy += 1
            out += Y[iy:]
            return out

        replay(rec(stageABC, 0))
        for j in range(16):
            X = rec(stageDE, j)
            Y = rec(stageABC, j + 1) if j + 1 < 16 else []
            replay(merge(X, Y))
        mark("l0_out")
        out_proj(wout0, nsub, RP, NT)

    def layer1(kind, nsub, RP, NT, nseq, T):
        samp = kind == "S"
        mark("l1_start")
        rmsnorm_T(kind, nsub, RP, 1)
        mark("l1_v")
        def vgrp(g4):
            slot = load_group(win1[:, g4 * 512:(g4 + 1) * 512], 512)
            for s in range(nsub):
                pb, pu = nextbank()
                for k in range(8):
                    S.op("pe", lambda e, k=k, s=s, pb=pb: e.matmul(pb[0:RP, :], lhsT=hT[:, k, s * 128:s * 128 + RP], rhs=wg[slot][:, k, :], start=(k == 0), stop=(k == 7)),
                         reads=["hT", ("wg", slot)], writes=[pu])
                col = s * 4 + g4
                if samp:
                    dst = gvs[0:RP, g4 * 512:(g4 + 1) * 512]
                    S.op("act", lambda e, pb=pb, dst=dst, col=col: e.activation(out=dst, in_=pb[0:RP, :], func=AF.Gelu_apprx_tanh, accum_out=stat[0:RP, col:col + 1]),
                         reads=[pu], writes=["gv", ("stat", "a")])
                    S.op("act", lambda e, dst=dst, col=col: e.activation(out=gsc2[0:RP, :], in_=dst, func=AF.Square, accum_out=stat[0:RP, 16 + col:17 + col]),
                         reads=["gv"], writes=["gsc2", ("stat", "b")])
                else:
                    S.op("act", lambda e, pb=pb, col=col: e.activation(out=gsc[0:RP, :], in_=pb[0:RP, :], func=AF.Gelu_apprx_tanh, accum_out=stat[0:RP, col:col + 1]),
                         reads=[pu], writes=["gsc", ("stat", "a")])
                    S.op("act", lambda e, col=col: e.activation(out=gsc2[0:RP, :], in_=gsc[0:RP, :], func=AF.Square, accum_out=stat[0:RP, 16 + col:17 + col]),
                         reads=["gsc"], writes=["gsc2", ("stat", "b")])
                    S.op("dve", lambda e, s=s, g4=g4: e.tensor_copy(out=gvb[0:RP, s * 2048 + g4 * 512:s * 2048 + (g4 + 1) * 512], in_=gsc[0:RP, :]), reads=["gsc"], writes=["gv"])
        for g4 in range(4):
            vgrp(g4)
        st4 = stat[0:RP, 0:4 * nsub].rearrange("p (s g) -> p s g", g=4)
        sq4 = stat[0:RP, 16:16 + 4 * nsub].rearrange("p (s g) -> p s g", g=4)
        S.op("dve", lambda e: e.tensor_reduce(out=stat[0:RP, 32:32 + nsub], in_=st4, axis=AX.X, op=ALU.add), reads=[("stat", "a")], writes=[("stat", 32)])
        S.op("dve", lambda e: e.tensor_reduce(out=stat[0:RP, 40:40 + nsub], in_=sq4, axis=AX.X, op=ALU.add), reads=[("stat", "b")], writes=[("stat", 40)])
        S.op("dve", lambda e: e.tensor_scalar(out=stat[0:RP, 32:32 + nsub], in0=stat[0:RP, 32:32 + nsub], scalar1=1.0 / E, scalar2=None, op0=ALU.mult), reads=[("stat", 32)], writes=[("stat", 32)])
        S.op("dve", lambda e: e.tensor_tensor(out=stat[0:RP, 48:48 + nsub], in0=stat[0:RP, 32:32 + nsub], in1=stat[0:RP, 32:32 + nsub], op=ALU.mult), reads=[("stat", 32)], writes=[("stat", 48)])
        S.op("dve", lambda e: e.scalar_tensor_tensor(out=stat[0:RP, 40:40 + nsub], in0=stat[0:RP, 40:40 + nsub], scalar=1.0 / E, in1=stat[0:RP, 48:48 + nsub], op0=ALU.mult, op1=ALU.subtract),
             reads=[("stat", 40), ("stat", 48)], writes=[("stat", 40)])
        S.op("dve", lambda e: e.tensor_scalar(out=stat[0:RP, 40:40 + nsub], in0=stat[0:RP, 40:40 + nsub], scalar1=1e-5, scalar2=None, op0=ALU.add), reads=[("stat", 40)], writes=[("stat", 40)])
        S.op("act", lambda e: e.activation(out=stat[0:RP, 40:40 + nsub], in_=stat[0:RP, 40:40 + nsub], func=AF.Sqrt), reads=[("stat", 40)], writes=[("stat", 40)])
        S.op("dve", lambda e: e.reciprocal(out=stat[0:RP, 40:40 + nsub], in_=stat[0:RP, 40:40 + nsub]), reads=[("stat", 40)], writes=[("stat", 40)])
        S.op("dve", lambda e: e.scalar_tensor_tensor(out=stat[0:RP, 48:48 + nsub], in0=stat[0:RP, 32:32 + nsub], scalar=-1.0, in1=stat[0:RP, 40:40 + nsub], op0=ALU.mult, op1=ALU.mult),
             reads=[("stat", 32), ("stat", 40)], writes=[("stat", 48)])
        for s in range(nsub):
            if samp:
                vsrc = gvs[0:RP, :]
                vmid = gvs[0:RP, :]
                vdst = gvs[0:RP, :]
            else:
                vsrc = gvb[0:RP, s * 2048:(s + 1) * 2048]
                vmid = t32[0:RP, :]
                vdst = vnb[0:RP, s * 2048:(s + 1) * 2048]
            S.op("dve", lambda e, s=s, vsrc=vsrc, vmid=vmid: e.tensor_scalar(out=vmid, in0=vsrc, scalar1=stat[0:RP, 40 + s:41 + s], scalar2=stat[0:RP, 48 + s:49 + s], op0=ALU.mult, op1=ALU.add),
                 reads=["gv", ("stat", 40), ("stat", 48)], writes=["t32"])
            S.op("dve", lambda e, vmid=vmid: e.tensor_tensor(out=vmid, in0=vmid, in1=vgbc[0:RP, 0, :], op=ALU.mult), reads=["t32", "vgbc"], writes=["t32"])
            S.op("dve", lambda e, vmid=vmid, vdst=vdst: e.tensor_tensor(out=vdst, in0=vmid, in1=vgbc[0:RP, 1, :], op=ALU.add), reads=["t32", "vgbc"], writes=["vn"])
        if samp:
            S.op("sp", lambda e: e.dma_start(out=sv, in_=gvs[0:64, :]), reads=["vn"], stream="out")
            S.op("act", lambda e: e.activation(out=gvb[0:64, 0:2048], in_=gvs[0:64, :], func=AF.Copy), reads=["vn"], writes=["gvb2"])
        mark("l1_uz")
        def uzgrp(jj):
            slot = load_group(win1[:, 2048 + jj * 512:2048 + (jj + 1) * 512], 512)
            for q in range(2):
                j = 2 * jj + q
                g = j // 2
                pq = j % 2
                gu, zt1, gz1, o1t = gu2[pq], zt12[pq], gz12[pq], o1t2[pq]
                banks = [nextbank(), nextbank()]
                for i in range(2):
                    pb, pu = banks[i]
                    for k in range(8):
                        S.op("pe", lambda e, i=i, k=k, q=q, pb=pb: e.matmul(pb[:, 0:NT], lhsT=wg[slot][:, k, (q * 2 + i) * 128:(q * 2 + i + 1) * 128], rhs=hT[:, k, 0:NT], start=(k == 0), stop=(k == 7)),
                             reads=["hT", ("wg", slot)], writes=[pu])
                (pbu, puu), (pbz, puz) = banks
                S.op("act", lambda e, pbu=pbu, gu=gu: e.activation(out=gu[:, 0:NT], in_=pbu[:, 0:NT], func=AF.Gelu_apprx_tanh), reads=[puu], writes=[("gu", pq)])
                S.op("act", lambda e, pbz=pbz, zt1=zt1: e.activation(out=zt1[:, 0:NT], in_=pbz[:, 0:NT], func=AF.Tanh, scale=0.5), reads=[puz], writes=[("zt1", pq)])
                S.op("dve", lambda e, pbz=pbz, zt1=zt1, gz1=gz1: e.scalar_tensor_tensor(out=gz1[:, 0:NT], in0=zt1[:, 0:NT], scalar=1.0, in1=pbz[:, 0:NT], op0=ALU.add, op1=ALU.mult),
                     reads=[("zt1", pq), puz], writes=[("gz1", pq)])
                pbm, pum = nextbank()
                if samp:
                    S.op("pe", lambda e, j=j, g=g, pbm=pbm: e.matmul(pbm[:, 0:64], lhsT=gvb[0:64, j * 128:(j + 1) * 128], rhs=bdtb[0:64, g, :], start=True, stop=False), reads=["gvb2", "bdtb"], writes=[pum])
                    S.op("pe", lambda e, g=g, pbm=pbm: e.matmul(pbm[:, 0:64], lhsT=onesr[0:1, :], rhs=bsS[0:1, g, :], start=False, stop=True), reads=["onesr", "bsS"], writes=[pum])
                else:
                    for s in range(nsub):
                        S.op("pe", lambda e, j=j, g=g, s=s, pbm=pbm: e.matmul(pbm[:, s * 128:(s + 1) * 128], lhsT=vnb[:, s * 2048 + j * 128:s * 2048 + (j + 1) * 128], rhs=wmT[:, g, :], start=True, stop=False),
                             reads=["vn", "wmT"], writes=[pum])
                        S.op("pe", lambda e, g=g, s=s, pbm=pbm: e.matmul(pbm[:, s * 128:(s + 1) * 128], lhsT=onesr[0:1, :], rhs=bsP[0:1, g * 128:(g + 1) * 128], start=False, stop=True),
                             reads=["onesr", "bsP"], writes=[pum])
                S.op("dve", lambda e, pbm=pbm, gu=gu, o1t=o1t: e.tensor_tensor(out=o1t[:, 0:NT], in0=pbm[:, 0:NT], in1=gu[:, 0:NT], op=ALU.mult), reads=[pum, ("gu", pq)], writes=[("o1t", pq)])
                S.op("dve", lambda e, j=j, o1t=o1t, gz1=gz1: e.tensor_tensor(out=ofm[:, j, 0:NT], in0=o1t[:, 0:NT], in1=gz1[:, 0:NT], op=ALU.mult), reads=[("o1t", pq), ("gz1", pq)], writes=["ofm"])
        for jj in range(8):
            uzgrp(jj)
        mark("l1_out")
        out_proj(wout1, nsub, RP, NT)
        mark("l1_fin")

    def final_norm(kind, nsub, RP, tok0):
        dst = ys if kind == "S" else yp
        for s in range(nsub):
            S.op("act", lambda e, s=s: e.activation(out=hb[0:RP, :], in_=xres[0:RP, s, :], func=AF.Square, accum_out=stat[0:RP, s:s + 1]),
                 reads=[("xres", s)], writes=["hb", ("stat", s)])
            S.op("dve", lambda e, s=s: e.tensor_scalar(out=stat[0:RP, 8 + s:9 + s], in0=stat[0:RP, s:s + 1], scalar1=1.0 / D, scalar2=1e-6, op0=ALU.mult, op1=ALU.add),
                 reads=[("stat", s)], writes=[("stat", 8 + s)])
            S.op("act", lambda e, s=s: e.activation(out=stat[0:RP, 16 + s:17 + s], in_=stat[0:RP, 8 + s:9 + s], func=AF.Sqrt), reads=[("stat", 8 + s)], writes=[("stat", 16 + s)])
            S.op("dve", lambda e, s=s: e.reciprocal(out=stat[0:RP, 24 + s:25 + s], in_=stat[0:RP, 16 + s:17 + s]), reads=[("stat", 16 + s)], writes=[("stat", 24 + s)])
            S.op("dve", lambda e, s=s: e.scalar_tensor_tensor(out=xres[0:RP, s, :], in0=xres[0:RP, s, :], scalar=stat[0:RP, 24 + s:25 + s], in1=gbc[0:RP, 2, :], op0=ALU.mult, op1=ALU.mult),
                 reads=[("xres", s), ("stat", 24 + s), "gbc"], writes=[("xres", s)])
            S.op("sp", lambda e, s=s: e.dma_start(out=dst[tok0 + s * 128:tok0 + s * 128 + RP, :], in_=xres[0:RP, s, :]), reads=[("xres", s)], stream="out")

    for tname in tiles:
        if tname == "S":
            kind, nsub, RP, NT, nseq, T, tok0 = "S", 1, 64, 64, 16, 4, 0
            S.fence()
            S.op("sp", lambda e: e.dma_start(out=sTM[0:16, :], in_=sshift), writes=["sTM"], stream="sin2")
            for cidx in range(50):
                if cidx < 48:
                    col0, w = (cidx // 16) * 2048 + (cidx % 16) * 128, 128
                else:
                    col0, w = 6144 + (cidx - 48) * 64, 64
                S.op("pe", lambda e, col0=col0, w=w: e.transpose(out=pm[0:w, 0:16], in_=sTM[0:16, col0:col0 + w], identity=cst[0:16, O_ID:O_ID + 16]), reads=["sTM", "cst"], writes=["pm"])
                S.op("act", lambda e, cidx=cidx, w=w: e.activation(out=sinT[0:w, cidx, :], in_=pm[0:w, 0:16], func=AF.Copy), reads=["pm"], writes=["sinT"])
            S.fence()
            src = xs
        else:
            kind, nsub, RP, NT, nseq, T = "P", 4, 128, 512, 1, 512
            tok0 = int(tname[1]) * 512
            src = xp
        for s in range(nsub):
            S.op("sp", lambda e, s=s, src=src, tok0=tok0, RP=RP: e.dma_start(out=xres[0:RP, s, :], in_=src[tok0 + s * 128:tok0 + s * 128 + RP, :]), writes=[("xres", s)], stream="xin")
        layer0(kind, nsub, RP, NT, nseq, T)
        S.fence()
        if _DEBUG.get("l0_only"):
            dst_ = ys if kind == "S" else yp
            for s in range(nsub):
                S.op("sp", lambda e, s=s, dst_=dst_, tok0=tok0, RP=RP: e.dma_start(out=dst_[tok0 + s * 128:tok0 + s * 128 + RP, :], in_=xres[0:RP, s, :]), reads=[("xres", s)], stream="out")
            continue
        layer1(kind, nsub, RP, NT, nseq, T)
        final_norm(kind, nsub, RP, tok0)
        S.fence()

    S.op("sp", lambda e: e.dma_start(out=pshift, in_=carryP[:, :, 0]), reads=["carryP"], stream="out")
    S.op("dve", lambda e: e.tensor_copy(out=SPf.rearrange("p (j v) -> p j v", v=64), in_=SPb[:, 0, :, :]),
         reads=[("SPb", j, 0) for j in range(16)], writes=["SPf"])
    for half in range(2):
        for jj in range(8):
            j = half * 8 + jj
            for h in range(2):
                hs = slice(64 * h, 64 * h + 64)
                S.op("pe", lambda e, hs=hs, h=h, j=j, jj=jj, half=half: e.matmul(pj[half][hs, jj * 64:(jj + 1) * 64], lhsT=SPf[hs, j * 64:(j + 1) * 64], rhs=cst[hs, O_ID + 64 * h:O_ID + 64 * h + 64], start=True, stop=True,
                                                                                 tile_position=(64 * h, 64 * h)), reads=["SPf", "cst"], writes=[("pj", half)])
        S.op("act", lambda e, half=half: e.activation(out=SnatE[:, half * 512:(half + 1) * 512], in_=pj[half][:, :], func=AF.Copy), reads=[("pj", half)], writes=["SnatE"])
    S.op("sp", lambda e: e.dma_start(out=pwkv.rearrange("(j h) v k -> (h v) j k", h=2), in_=SnatE.rearrange("p (j k) -> p j k", k=64)), reads=["SnatE"], stream="out")
    if "S" in tiles:
        for cidx in range(50):
            if cidx < 48:
                col0, w = (cidx // 16) * 2048 + (cidx % 16) * 128, 128
            else:
                col0, w = 6144 + (cidx - 48) * 64, 64
            S.op("pe", lambda e, cidx=cidx, w=w: e.transpose(out=pm[0:16, 0:w], in_=soutT[0:w, cidx, :], identity=cst[0:w, O_ID:O_ID + w]), reads=["soutT", "cst"], writes=["pm"])
            S.op("act", lambda e, col0=col0, w=w: e.activation(out=sTM[0:16, col0:col0 + w], in_=pm[0:16, 0:w], func=AF.Copy), reads=["pm"], writes=["sTM"])
        S.op("sp", lambda e: e.dma_start(out=sshift_o, in_=sTM[0:16, :]), reads=["sTM"], stream="out")
    _DEBUG["nops"] = len(S.all_ops)
    _DEBUG["marks"] = marks
    S.emit(final_wait_streams=["out"] if "out" in S.streams else [])
    _DEBUG["sigcnts"] = S.sigcnts
    _DEBUG["streams"] = {k: v[0] for k, v in S.streams.items()}
    return nc


def make_consts():
    c = np.zeros((128, NCST), np.float32)
    c[:, O_ID:O_ID + 128] = np.eye(128)
    for h in range(2):
        c[64 * h:64 * h + 64, O_BO + 64 * h:O_BO + 64 * h + 64] = 1.0
    i = np.arange(64)[:, None]
    t = np.arange(64)[None, :]
    su = (i < t).astype(np.float32)
    ui = (i <= t).astype(np.float32)
    sl = (t < i).astype(np.float32)
    same = ((i // 4) == (t // 4)).astype(np.float32)
    for h in range(2):
        r = slice(64 * h, 64 * h + 64)
        c[r, O_M2P:O_M2P + 64] = su
        c[r, O_M2P + 64:O_M2P + 128] = ui
        c[r, O_MLP:O_MLP + 64] = sl
        c[r, O_M2S:O_M2S + 64] = su * same
        c[r, O_M2S + 64:O_M2S + 128] = ui * same
        c[r, O_MLS:O_MLS + 64] = sl * same
    rp = np.ones(512, np.float32)
    rp[::64] = 0
    c[:, O_RSTP:O_RSTP + 512] = rp
    rs = np.ones(64, np.float32)
    rs[::4] = 0
    c[:, O_RSTS:O_RSTS + 64] = rs
    seq = np.zeros((16, 64), np.float32)
    for b in range(16):
        seq[b, 4 * b:4 * b + 4] = 1
    c[:, O_SEQ:O_SEQ + 1024] = seq.reshape(1, 1024)
    p = np.arange(128)
    row = np.zeros((128, 16), np.float32)
    row[p, (p % 64) // 4] = 1
    c[:, O_ROW:O_ROW + 16] = row
    s_ = np.arange(128)[:, None]
    t_ = np.arange(128)[None, :]
    c[:, O_TRIU:O_TRIU + 128] = (s_ <= t_).astype(np.float32)
    return c


_NC_CACHE = {}
_DEBUG = {}


def kernel(x_prompt, x_sample, state_shift, state_wkv, norm_g, norm_f, rw_in, rw_mu, rw_w0, rw_w2, rw_a0, rw_a2,
           rw_kk, rw_ka, rw_rk, rw_lnx_g, rw_lnx_b, rw_out, gm_in, gm_vg, gm_vb, gm_ws, gm_bs, gm_out):
    f = lambda a: np.ascontiguousarray(np.asarray(a, dtype=np.float32))
    x_prompt, x_sample, state_shift, state_wkv = f(x_prompt), f(x_sample), f(state_shift), f(state_wkv)
    perm0 = list(range(6144, 6272))
    for j in range(16):
        for base in (0, 2048, 4096, 6272):
            perm0 += list(range(base + j * 128, base + (j + 1) * 128))
    win0 = f(f(rw_in)[0][:, perm0])
    perm1 = list(range(2048, 4096))
    for j in range(16):
        for base in (0, 4096):
            perm1 += list(range(base + j * 128, base + (j + 1) * 128))
    win1 = f(f(gm_in)[0][:, perm1])
    fm = lambda v: f(v).reshape(16, 128).T
    mu = f(rw_mu)[0]
    ppn = np.zeros((128, 130), np.float32)
    ppn[:, 0:16] = fm(mu[0:2048])
    ppn[:, 16:32] = fm(mu[2048:4096])
    ppn[:, 32:48] = fm(mu[4096:6144])
    ppn[:, 48:64] = fm(f(rw_kk)[0])
    ppn[:, 64:80] = fm(f(rw_ka)[0])
    ppn[:, 80:96] = fm(f(rw_lnx_g)[0])
    ppn[:, 96:112] = fm(f(rw_lnx_b)[0])
    ppn[:, 112:128] = fm(f(rw_rk)[0].reshape(-1))
    ppn[0:64, 128] = mu[6144:6208]
    ppn[0:64, 129] = mu[6208:6272]
    w2aug = f(np.concatenate([f(rw_w2)[0], f(rw_w0)], axis=0))
    a2aug = f(np.concatenate([f(rw_a2)[0], f(rw_a0)], axis=0))
    gnn = f(np.concatenate([f(norm_g), f(norm_f)[None, :]], axis=0))
    vgbn = f(np.concatenate([f(gm_vg), f(gm_vb)], axis=0))
    shared = {
        "gn": gnn, "win0": win0, "pp": ppn, "w2aug": w2aug, "a2aug": a2aug, "wout0": f(f(rw_out)[0]),
        "win1": win1, "vgb": vgbn, "ws": f(f(gm_ws)[0]), "bs": f(f(gm_bs)[0].reshape(1, 1024)),
        "wout1": f(f(gm_out)[0]), "cst": make_consts(),
    }
    in_maps = []
    for c in range(NCORES):
        m = dict(shared)
        m["xp"] = f(x_prompt[c])
        m["xs"] = f(x_sample[16 * c:16 * c + 16].reshape(64, D))
        m["sshift"] = f(state_shift[0, 16 * c:16 * c + 16])
        m["swkv"] = f(state_wkv[0, 16 * c:16 * c + 16])
        in_maps.append(m)
    if _DEBUG.get("maps_only"):
        return in_maps
    if "nc" not in _NC_CACHE:
        _NC_CACHE["nc"] = build_nc()
    nc = _NC_CACHE["nc"]
    res = run_bass_kernel_spmd(nc, in_maps, core_ids=list(range(NCORES)))
    R = res.results
    y_prompt = np.stack([R[c]["yp"] for c in range(NCORES)]).astype(np.float32)
    y_sample = np.concatenate([R[c]["ys"].reshape(16, 4, D) for c in range(NCORES)]).astype(np.float32)
    pshift = np.zeros((1, NCORES, SHIFT), np.float32)
    for c in range(NCORES):
        ps = R[c]["pshift"]
        for i in range(3):
            pshift[0, c, i * 2048:(i + 1) * 2048] = ps[:, i * 16:(i + 1) * 16].T.reshape(-1)
        pshift[0, c, 6144:6208] = ps[0:64, 48]
        pshift[0, c, 6208:6272] = ps[0:64, 49]
    prompt_wkv = np.stack([R[c]["pwkv"] for c in range(NCORES)])[None].astype(np.float32)
    sample_shift = np.concatenate([R[c]["sshift_o"] for c in range(NCORES)])[None].astype(np.float32)
    sample_wkv = np.concatenate([R[c]["swkv_o"] for c in range(NCORES)])[None].astype(np.float32)
    sample_v = np.concatenate([R[c]["sv"].reshape(16, 4, E) for c in range(NCORES)])[None].astype(np.float32)
    return (y_prompt, y_sample, pshift, prompt_wkv, sample_shift, sample_wkv, sample_v)
```
